# Optimizing a Trainium2 kernel written in Bass

```python
import jax, jax.numpy as jnp
from jax import lax
import numpy as np

D_MODEL = 1024
BATCH = 8
SEQ = 2048
DEPTH = 4
DEC_BATCH = 128
DEC_SEQ = 4
PAST_LEN = 16384
PAGE_SIZE = 128

W_A = D_MODEL
W_B = D_MODEL
W_C = D_MODEL
N_BRANCH = 3
CONV_A_WIDTH = 31
CONV_B_WIDTH = 4
CONV_C_WIDTH = 3
LRU_BLOCKS = 16
LRU_BLOCK = W_B // LRU_BLOCKS
LRU_C = 8.0
EPS = 1e-6
SPLIT_SIZES = (W_A, W_A, W_A, W_B, W_B, W_C, W_C, W_C, W_C, N_BRANCH * D_MODEL)
IN_COLS = sum(SPLIT_SIZES)
SPLIT_POINTS = tuple(int(s) for s in np.cumsum(SPLIT_SIZES)[:-1])

kernel_name = "hybrid_gated_conformer_rglru_shortconv_step"


def rmsnorm(x, g):
    xf = x.astype(jnp.float32)
    y = xf * lax.rsqrt(jnp.mean(xf * xf, axis=-1, keepdims=True) + EPS)
    return (y * g.astype(jnp.float32)).astype(x.dtype)


def layernorm(x, g, b):
    xf = x.astype(jnp.float32)
    mu = jnp.mean(xf, axis=-1, keepdims=True)
    var = jnp.mean(jnp.square(xf - mu), axis=-1, keepdims=True)
    y = (xf - mu) * lax.rsqrt(var + EPS)
    return (y * g.astype(jnp.float32) + b.astype(jnp.float32)).astype(x.dtype)


def causal_dwconv(u, buf, w):
    k = w.shape[0]
    full = jnp.concatenate([buf.astype(u.dtype), u], axis=1)
    out = lax.conv_general_dilated(full, w.astype(u.dtype)[:, None, :], window_strides=(1,),
                                   padding='VALID', dimension_numbers=('NWC', 'WIO', 'NWC'),
                                   feature_group_count=u.shape[-1])
    new_buf = full[:, full.shape[1] - (k - 1):]
    return out, new_buf


def rglru(x, h0, w_a, b_a, w_x, b_x, lam):
    bsz, t, w = x.shape
    xh = x.reshape(bsz, t, LRU_BLOCKS, LRU_BLOCK)
    r = jax.nn.sigmoid(jnp.einsum('bthi,hij->bthj', xh, w_a).reshape(bsz, t, w) + b_a)
    i = jax.nn.sigmoid(jnp.einsum('bthi,hij->bthj', xh, w_x).reshape(bsz, t, w) + b_x)
    log_a = -LRU_C * r.astype(jnp.float32) * jax.nn.softplus(-lam.astype(jnp.float32))
    a = jnp.exp(log_a)
    u = jnp.sqrt(-jnp.expm1(2.0 * log_a)) * (i * x).astype(jnp.float32)

    def step(h, inp):
        a_t, u_t = inp
        h = a_t * h + u_t
        return h, h

    h_last, hs = lax.scan(step, h0.astype(jnp.float32), (jnp.swapaxes(a, 0, 1), jnp.swapaxes(u, 0, 1)))
    return jnp.swapaxes(hs, 0, 1).astype(x.dtype), h_last.astype(h0.dtype)


def hybrid_layer(x, buf_a, buf_b, h_b, buf_c, p):
    bsz, t, _ = x.shape
    h = rmsnorm(x, p['norm_pre'])
    proj = jnp.einsum('btd,dc->btc', h, p['w_in']) + p['b_in']
    a_v, a_g, a_z, b_x, b_z, c_b, c_c, c_x, c_z, gates = jnp.split(proj, SPLIT_POINTS, axis=-1)

    ua = a_v * jax.nn.sigmoid(a_g)
    ua, nbuf_a = causal_dwconv(ua, buf_a, p['conv_a_w'])
    ua = jax.nn.silu(layernorm(ua + p['conv_a_b'], p['ln_a_g'], p['ln_a_b']))
    y_a = jnp.einsum('btc,cd->btd', ua * jax.nn.silu(a_z), p['w_a_out'])

    xb, nbuf_b = causal_dwconv(b_x, buf_b, p['conv_b_w'])
    xb = xb + p['conv_b_b']
    yb, nh_b = rglru(xb, h_b, p['rg_w_a'], p['rg_b_a'], p['rg_w_x'], p['rg_b_x'], p['rg_lambda'])
    y_b = jnp.einsum('btc,cd->btd', yb * jax.nn.silu(b_z), p['w_b_out'])

    cx, nbuf_c = causal_dwconv(c_c * c_x, buf_c, p['conv_c_w'])
    y_c = jnp.einsum('btc,cd->btd', c_b * cx * jax.nn.silu(c_z), p['w_c_out'])

    g = jax.nn.sigmoid(gates).reshape(bsz, t, N_BRANCH, D_MODEL)
    merged = g[:, :, 0] * y_a + g[:, :, 1] * y_b + g[:, :, 2] * y_c
    out = jnp.einsum('btd,de->bte', merged, p['w_o'])
    return x + rmsnorm(out, p['norm_post']), nbuf_a, nbuf_b, nh_b, nbuf_c


def setup_inputs(seed: int = 0) -> dict:
    key = jax.random.key(seed)
    ks = jax.random.split(key, 32)

    def nrm(k, shape, scale):
        return jax.random.normal(k, shape, jnp.float32) * scale

    a0 = jax.random.uniform(ks[22], (DEPTH, W_B), jnp.float32, 0.9, 0.999)
    return {
        'x_prompt': nrm(ks[0], (BATCH, SEQ, D_MODEL), 1.0),
        'x_sample': nrm(ks[1], (DEC_BATCH, DEC_SEQ, D_MODEL), 1.0),
        'state_conv_a': nrm(ks[2], (DEPTH, DEC_BATCH, CONV_A_WIDTH - 1, W_A), 0.5),
        'state_conv_b': nrm(ks[3], (DEPTH, DEC_BATCH, CONV_B_WIDTH - 1, W_B), 1.0),
        'state_lru': nrm(ks[4], (DEPTH, DEC_BATCH, W_B), 0.5),
        'state_conv_c': nrm(ks[5], (DEPTH, DEC_BATCH, CONV_C_WIDTH - 1, W_C), 0.5),
        'norm_pre': 1.0 + nrm(ks[6], (DEPTH, D_MODEL), 0.02),
        'norm_post': 1.0 + nrm(ks[7], (DEPTH, D_MODEL), 0.02),
        'w_in': nrm(ks[8], (DEPTH, D_MODEL, IN_COLS), D_MODEL ** -0.5),
        'b_in': nrm(ks[9], (DEPTH, IN_COLS), 0.02),
        'conv_a_w': nrm(ks[10], (DEPTH, CONV_A_WIDTH, W_A), CONV_A_WIDTH ** -0.5),
        'conv_a_b': nrm(ks[11], (DEPTH, W_A), 0.02),
        'ln_a_g': 1.0 + nrm(ks[12], (DEPTH, W_A), 0.02),
        'ln_a_b': nrm(ks[13], (DEPTH, W_A), 0.02),
        'w_a_out': nrm(ks[14], (DEPTH, W_A, D_MODEL), W_A ** -0.5),
        'conv_b_w': nrm(ks[15], (DEPTH, CONV_B_WIDTH, W_B), CONV_B_WIDTH ** -0.5),
        'conv_b_b': nrm(ks[16], (DEPTH, W_B), 0.02),
        'rg_w_a': nrm(ks[17], (DEPTH, LRU_BLOCKS, LRU_BLOCK, LRU_BLOCK), LRU_BLOCK ** -0.5),
        'rg_b_a': nrm(ks[18], (DEPTH, W_B), 0.02),
        'rg_w_x': nrm(ks[19], (DEPTH, LRU_BLOCKS, LRU_BLOCK, LRU_BLOCK), LRU_BLOCK ** -0.5),
        'rg_b_x': nrm(ks[20], (DEPTH, W_B), 0.02),
        'rg_lambda': jnp.log(a0) - jnp.log1p(-a0),
        'w_b_out': nrm(ks[21], (DEPTH, W_B, D_MODEL), W_B ** -0.5),
        'conv_c_w': nrm(ks[23], (DEPTH, CONV_C_WIDTH, W_C), CONV_C_WIDTH ** -0.5),
        'w_c_out': nrm(ks[24], (DEPTH, W_C, D_MODEL), W_C ** -0.5),
        'w_o': nrm(ks[25], (DEPTH, D_MODEL, D_MODEL), D_MODEL ** -0.5),
    }


def reference(x_prompt, x_sample, state_conv_a, state_conv_b, state_lru, state_conv_c,
              norm_pre, norm_post, w_in, b_in, conv_a_w, conv_a_b, ln_a_g, ln_a_b, w_a_out,
              conv_b_w, conv_b_b, rg_w_a, rg_b_a, rg_w_x, rg_b_x, rg_lambda, w_b_out,
              conv_c_w, w_c_out, w_o):
    bp = x_prompt.shape[0]
    dt = x_prompt.dtype
    xp, xs = x_prompt, x_sample
    pa, pb, ph, pc = [], [], [], []
    sa, sb, sh, sc = [], [], [], []
    for l in range(DEPTH):
        p = {
            'norm_pre': norm_pre[l], 'norm_post': norm_post[l], 'w_in': w_in[l], 'b_in': b_in[l],
            'conv_a_w': conv_a_w[l], 'conv_a_b': conv_a_b[l], 'ln_a_g': ln_a_g[l], 'ln_a_b': ln_a_b[l],
            'w_a_out': w_a_out[l], 'conv_b_w': conv_b_w[l], 'conv_b_b': conv_b_b[l],
            'rg_w_a': rg_w_a[l], 'rg_b_a': rg_b_a[l], 'rg_w_x': rg_w_x[l], 'rg_b_x': rg_b_x[l],
            'rg_lambda': rg_lambda[l], 'w_b_out': w_b_out[l], 'conv_c_w': conv_c_w[l],
            'w_c_out': w_c_out[l], 'w_o': w_o[l],
        }
        xp, na, nb, nh, nc = hybrid_layer(
            xp,
            jnp.zeros((bp, CONV_A_WIDTH - 1, W_A), dt),
            jnp.zeros((bp, CONV_B_WIDTH - 1, W_B), dt),
            jnp.zeros((bp, W_B), dt),
            jnp.zeros((bp, CONV_C_WIDTH - 1, W_C), dt),
            p)
        pa.append(na); pb.append(nb); ph.append(nh); pc.append(nc)
        xs, na, nb, nh, nc = hybrid_layer(
            xs, state_conv_a[l], state_conv_b[l], state_lru[l], state_conv_c[l], p)
        sa.append(na); sb.append(nb); sh.append(nh); sc.append(nc)
    return (xp, xs,
            jnp.stack(pa), jnp.stack(pb), jnp.stack(ph), jnp.stack(pc),
            jnp.stack(sa), jnp.stack(sb), jnp.stack(sh), jnp.stack(sc))
```

```python
import contextlib
import numpy as np
import concourse.bass as bass
import concourse.mybir as mybir
from concourse.bass_utils import run_bass_kernel_spmd

F32 = mybir.dt.float32
BF16 = mybir.dt.bfloat16
AF = mybir.ActivationFunctionType
ALU = mybir.AluOpType

NCORES = 8
D = 1024
NCH = 8
IN_COLS = 12288
NSEQ = 16
EPS = 1e-6
NLAYERS = 4
TILES = [(0, 512, 'p'), (512, 512, 'p'), (1024, 512, 'p'), (1536, 512, 'p'), (2048, 64, 's')]
PASSES = [[0], [1], [2], [3, 4]]
TPB = 576
T_ALL = 2112
NVEC = 59
NSOUT = 196
NRING = 6
DG_ENG = "dve"
GC = 256

V_NPRE, V_NPOST, V_BIN, V_CAW, V_CAB, V_LNG, V_LNB, V_CBW, V_CBB, V_RBA, V_RBX, V_LAM, V_CCW = \
    0, 1, 2, 14, 45, 46, 47, 48, 52, 53, 54, 55, 56
C_AV, C_AG, C_AZ, C_BX, C_BZ, C_CB, C_CC, C_CX, C_CZ, C_G = \
    0, 1024, 2048, 3072, 4096, 5120, 6144, 7168, 8192, 9216


class Ins:
    __slots__ = ("eng", "fn", "deps", "idx", "inc", "val", "is_dma", "sem", "semval")


class Tracker:
    ENGS = ("pe", "act", "dve", "pool", "sp")

    def __init__(self):
        self.lists = {e: [] for e in self.ENGS}
        self.lastw = {}
        self.readers = {}
        self.dma_sems = {}

    def op(self, eng, fn, reads=(), writes=(), dma_sem=None):
        I = Ins()
        I.eng = eng
        I.fn = fn
        I.idx = len(self.lists[eng])
        I.inc = False
        I.val = 0
        I.is_dma = dma_sem is not None
        I.sem = dma_sem
        I.semval = 0
        deps = set()
        for r in reads:
            w = self.lastw.get(r)
            if w is not None:
                deps.add(w)
        for k in writes:
            w = self.lastw.get(k)
            if w is not None:
                deps.add(w)
            rd = self.readers.get(k)
            if rd:
                deps.update(rd.values())
        if dma_sem is not None:
            lst = self.dma_sems.setdefault(dma_sem, [])
            if lst:
                deps.add(lst[-1])
            I.semval = 16 * (len(lst) + 1)
            lst.append(I)
        deps.discard(I)
        I.deps = deps
        for d in deps:
            if not d.is_dma:
                d.inc = True
        for r in reads:
            rd = self.readers.setdefault(r, {})
            rd[("d", id(I)) if I.is_dma else eng] = I
        for k in writes:
            self.lastw[k] = I
            self.readers[k] = {}
        self.lists[eng].append(I)
        return I


def build_program(nl=NLAYERS, passes=PASSES, debug=False):
    nc = bass.Bass("TRN2", target_bir_lowering=False)
    T = Tracker()
    dbg_list = []
    marks = []

    def MARK(label):
        marks.append((label, len(T.lists['pe']), len(T.lists['act']), len(T.lists['dve'])))

    def DBG(name, ap, shape, dt, keys):
        if not debug:
            return
        dd = nc.dram_tensor("dbg_" + name, list(shape), dt, kind="ExternalOutput").ap()
        dbg_list.append(name)
        T.op("sp", lambda e: e.dma_start(out=dd, in_=ap), keys, [], dma_sem=dsem("dbg_" + name))

    def din(name, shape):
        return nc.dram_tensor(name, list(shape), F32, kind="ExternalInput").ap()

    def dout(name, shape):
        return nc.dram_tensor(name, list(shape), F32, kind="ExternalOutput").ap()

    xin = din("xin", [T_ALL, D])
    sa_in = din("sa_in", [NLAYERS, 30, NSEQ, D])
    sbc_in = din("sbc_in", [NLAYERS, 96, D])
    vecs = din("vecs", [256, D])
    ident_d = din("ident", [128, 128])
    w_in = din("w_in", [NLAYERS, D, IN_COLS])
    w_a_out = din("w_a_out", [NLAYERS, D, D])
    w_b_out = din("w_b_out", [NLAYERS, D, D])
    w_c_out = din("w_c_out", [NLAYERS, D, D])
    w_o = din("w_o", [NLAYERS, D, D])
    rg_w_a = din("rg_w_a", [NLAYERS, 16, 64, 64])
    rg_w_x = din("rg_w_x", [NLAYERS, 16, 64, 64])
    yout = dout("yout", [T_ALL, D])
    st_out = dout("st_out", [NLAYERS, NSOUT, D])
    sa_old = dout("sa_old", [NLAYERS, 26, NSEQ, D])
    dgs = nc.dram_tensor("dgs", [NLAYERS, NCH, 128, 31 * 128], BF16, kind="Internal").ap()

    es = contextlib.ExitStack()

    def sb(name, shape, dt):
        return es.enter_context(nc.sbuf_tensor(name, list(shape), dt))

    X = sb("X", [128, NCH, TPB], F32)
    H = sb("H", [128, NCH, TPB], BF16)
    R32 = sb("R32", [128, NCH * TPB], F32)
    PC = sb("PC", [128, NCH, TPB], BF16)
    M = sb("M", [128, NCH, TPB], BF16)
    RING = sb("RING", [128, NRING, NCH, GC], BF16)
    DG = sb("DG", [128, 2, 31, 128], BF16)
    NTF = 19
    NTB = 7
    TF = sb("TF", [128, NTF, 516], F32)
    TB = sb("TB", [128, NTB, 576], BF16)
    VT = sb("VT", [128, NCH, 256], F32)
    CL = sb("CL", [128, NLAYERS, NCH], F32)
    CLT = sb("CLT", [128, NLAYERS, NCH], F32)
    CL2 = sb("CL2", [128, NLAYERS, NCH], F32)
    IDF = sb("IDF", [128, 128], F32)
    IDB = sb("IDB", [128, 128], BF16)
    ONESB = sb("ONESB", [128, 128], BF16)
    RG = sb("RG", [128, NLAYERS, NCH, 2, 128], BF16)
    ORW = sb("ORW", [128, 2, D], F32)
    SOUT = sb("SOUT", [128, NCH, NSOUT], F32)
    US = sb("US", [128, NCH, NSEQ, 34], BF16)
    BXS = sb("BXS", [128, NCH, NSEQ, 7], F32)
    CINS = sb("CINS", [128, NCH, NSEQ, 6], F32)
    LRUS = sb("LRUS", [128, NCH, NSEQ], F32)
    HA = sb("HA", [128, NLAYERS, NCH, 30], BF16)
    HB = sb("HB", [128, NLAYERS, NCH, 3], F32)
    HC = sb("HC", [128, NLAYERS, NCH, 2], F32)
    HL = sb("HL", [128, NLAYERS, NCH], F32)
    PS = es.enter_context(nc.psum_tensor("PS", [128, 8, 512], F32))

    Rb = R32[:].bitcast(BF16).rearrange("p (c t) -> p c t", t=TPB)
    O = R32[:].rearrange("p (c t) -> p c t", t=TPB)

    sems = {}
    for e in Tracker.ENGS:
        sems[("eng", e)] = es.enter_context(nc.semaphore("s_" + e))

    def dsem(name):
        k = ("dma", name)
        if k not in sems:
            sems[k] = es.enter_context(nc.semaphore("d_" + str(name)))
        return name

    def kw_act(bias, scale):
        kw = {}
        if bias is not None:
            kw["bias"] = bias
        if scale is not None:
            kw["scale"] = scale
        return kw

    def ACT(out, in_, func, reads, writes, bias=None, scale=None):
        kw = kw_act(bias, scale)
        return T.op("act", lambda e: e.activation(out=out, in_=in_, func=func, **kw), reads, writes)

    def TT(out, in0, in1, op, reads, writes, eng="dve"):
        return T.op(eng, lambda e: e.tensor_tensor(out, in0, in1, op), reads, writes)

    def STT(out, in0, scalar, in1, op0, op1, reads, writes, eng="dve"):
        return T.op(eng, lambda e: e.scalar_tensor_tensor(out, in0, scalar, in1, op0, op1), reads, writes)

    def TS(out, in0, s1, s2, op0, op1, reads, writes, eng="dve"):
        if s2 is None:
            return T.op(eng, lambda e: e.tensor_scalar(out, in0, s1, None, op0), reads, writes)
        return T.op(eng, lambda e: e.tensor_scalar(out, in0, s1, s2, op0, op1), reads, writes)

    def CP(out, in_, reads, writes, eng="dve"):
        if eng == "act":
            return T.op(eng, lambda e: e.copy(out, in_), reads, writes)
        return T.op(eng, lambda e: e.tensor_copy(out, in_), reads, writes)

    def MM(out, lhsT, rhs, start, stop, reads, writes):
        return T.op("pe", lambda e: e.matmul(out, lhsT, rhs, start=start, stop=stop), reads, writes)

    def TR(out, in_, ident, reads, writes):
        return T.op("pe", lambda e: e.transpose(out, in_, ident), reads, writes)

    def DMA(eng, out, in_, semname, reads, writes, **kw):
        dsem(semname)
        return T.op(eng, lambda e: e.dma_start(out=out, in_=in_, **kw), reads, writes, dma_sem=semname)

    misc_rr = [0]

    def misc_sem():
        misc_rr[0] = (misc_rr[0] + 1) % 8
        return "m%d" % misc_rr[0]

    bank_free = list(range(8))

    def bank():
        if not bank_free:
            raise RuntimeError("PSUM bank liveness exceeded 8")
        return bank_free.pop(0)

    def bfree(*bs):
        for b in bs:
            assert b not in bank_free
            bank_free.append(b)

    def hold_bank():
        return bank()

    def unhold(b):
        pass

    def psk(b):
        return ("ps", b)

    def tf(i, n, off=0):
        return TF[:, i, off:off + n]

    def tfk(i):
        return ("tf", i)

    def tb(i, n):
        return TB[:, i, 0:n]

    def tbk(i):
        return ("tb", i)

    def vt(l, row, c):
        return VT[:, c, l * NVEC + row: l * NVEC + row + 1]

    def b_in(l, colbase, c):
        j = colbase // 128 + c
        return VT[:, j % 8, l * NVEC + V_BIN + j // 8: l * NVEC + V_BIN + j // 8 + 1]

    def wsrc(name, l, col0):
        if name == "w_in":
            v = w_in[l].rearrange("(k p) c -> p k c", p=128)
        else:
            v = {"w_a_out": w_a_out, "w_b_out": w_b_out, "w_c_out": w_c_out, "w_o": w_o}[name][l] \
                .rearrange("(k p) c -> p k c", p=128)
        return v[:, :, col0:col0 + GC]

    ws = {"free": list(range(NRING)), "slot_of": {}}

    def ws_get(p_i, l, name, col0):
        key = (p_i, l, name, col0)
        if key not in ws["slot_of"]:
            slot = ws["free"].pop(0)
            ws["slot_of"][key] = slot
            DMA("pool", RING[:, slot], wsrc(name, l, col0), "wr%d" % slot, [], [("wr", slot)])
        return ws["slot_of"][key]

    def ws_release(p_i, l, name, col0):
        slot = ws["slot_of"].pop((p_i, l, name, col0))
        ws["free"].append(slot)

    DMA("sp", IDF[:], ident_d, misc_sem(), [], [("IDF",)])
    CP(IDB[:], IDF[:], [("IDF",)], [("IDB",)])
    T.op("dve", lambda e: e.memset(ONESB[:], 1.0), [], [("ONESB",)])
    T.op("dve", lambda e: e.memset(HA[:], 0.0), [], [("HA",)])
    T.op("dve", lambda e: e.memset(HB[:], 0.0), [], [("HB",)])
    T.op("dve", lambda e: e.memset(HC[:], 0.0), [], [("HC",)])
    T.op("dve", lambda e: e.memset(HL[:], 0.0), [], [("HL",)])
    T.op("dve", lambda e: e.memset(RG[:], 0.0), [], [("RG",)])

    for rt in range(2):
        DMA("sp", ORW[:, rt, :], vecs[rt * 128:(rt + 1) * 128, :], misc_sem(), [], [("ORW", rt)])
        for half in range(2):
            b = bank()
            for j in range(4):
                c = half * 4 + j
                TR(PS[:, b, j * 128:(j + 1) * 128], ORW[:, rt, c * 128:(c + 1) * 128], IDF[:],
                   [("ORW", rt), ("IDF",)], [psk(b)])
            ACT(VT[:, half * 4:half * 4 + 4, rt * 128:(rt + 1) * 128],
                PS[:, b, :].rearrange("p (j r) -> p j r", r=128), AF.Identity,
                [psk(b)], [("VT",)])
            bfree(b)
    for l in range(NLAYERS):
        lam = VT[:, :, l * NVEC + V_LAM]
        ACT(CLT[:, l, :], lam, AF.Exp, [("VT",)], [("CLT",)], scale=-1.0)
    ACT(CLT[:], CLT[:], AF.Ln, [("CLT",)], [("CLT",)], bias=1.0)
    TS(CL[:], CLT[:], -8.0, None, ALU.mult, None, [("CLT",)], [("CL",)])
    TS(CL2[:], CLT[:], -16.0, None, ALU.mult, None, [("CLT",)], [("CL",)])

    def rg_load(l):
        for gi, rw in enumerate((rg_w_a, rg_w_x)):
            for half in range(2):
                src = rw[l].rearrange("(c h) i j -> h i c j", h=2)[half]
                DMA("pool", RG[half * 64:(half + 1) * 64, l, :, gi, half * 64:(half + 1) * 64], src,
                    "rg%d" % (gi * 2 + half), [("RG",)], [("RG", l, gi, half)])

    def run_pipeline(gens):
        active = []
        for g in gens:
            active.append(g)
            for a in list(active):
                try:
                    next(a)
                except StopIteration:
                    active.remove(a)
        while active:
            for a in list(active):
                try:
                    next(a)
                except StopIteration:
                    active.remove(a)

    def proj(p_i, l, name, col0, cc, rhs_buf, rhs_keyf, off, nt):
        slot = ws_get(p_i, l, name, col0)
        b = bank()
        for k in range(NCH):
            MM(PS[:, b, 0:nt], RING[:, slot, k, cc * 128:(cc + 1) * 128], rhs_buf[:, k, off:off + nt],
               k == 0, k == NCH - 1, [("wr", slot), rhs_keyf(k)], [psk(b)])
        return b

    def proj_multi(p_i, l, name, col0, cc, rhs_buf, rhs_keyf, utiles):
        slot = ws_get(p_i, l, name, col0)
        banks = {t[0]: bank() for t in utiles}
        for k in range(NCH):
            for (ti, st, nt, kind, off) in utiles:
                MM(PS[:, banks[ti], 0:nt], RING[:, slot, k, cc * 128:(cc + 1) * 128], rhs_buf[:, k, off:off + nt],
                   k == 0, k == NCH - 1, [("wr", slot), rhs_keyf(k)], [psk(banks[ti])])
        return banks

    def stats_rs(b, nt, out_i):
        ACT(tf(out_i, nt), PS[:, b, 0:nt], AF.Ln, [psk(b)], [tfk(out_i)], bias=EPS, scale=1.0 / D)
        ACT(tf(out_i, nt), tf(out_i, nt), AF.Exp, [tfk(out_i)], [tfk(out_i)], scale=-0.5)

    for p_i, pss in enumerate(passes):
        tiles = []
        base = TILES[pss[0]][0]
        for ti in pss:
            st, nt, kind = TILES[ti]
            tiles.append((ti, st, nt, kind, st - base))
        is_last_pass = (pss[-1] == len(TILES) - 1)

        def Xk(c):
            return ("X", c)

        def Hk(c):
            return ("H", c)

        def Rk(j):
            return ("R", j)

        def PCk(c):
            return ("PC", c)

        def Mk(c):
            return ("M", c)

        for (ti, st, nt, kind, off) in tiles:
            for r0 in range(0, nt, 128):
                nr = min(128, nt - r0)
                slot = (r0 // 128) % 2
                DMA("sp", ORW[0:nr, slot, :], xin[st + r0: st + r0 + nr, :], misc_sem(), [], [("ORW", slot)])
                for half in range(2):
                    b = bank()
                    for j in range(4):
                        c = half * 4 + j
                        TR(PS[:, b, j * 128: j * 128 + nr], ORW[0:nr, slot, c * 128:(c + 1) * 128],
                           IDF[0:nr, 0:nr], [("ORW", slot), ("IDF",)], [psk(b)])
                    ACT(X[:, half * 4:half * 4 + 4, off + r0: off + r0 + nr],
                        PS[:, b, :].rearrange("p (j r) -> p j r", r=128)[:, :, 0:nr], AF.Identity,
                        [psk(b)], [Xk(half * 4 + j) for j in range(4)])
                    bfree(b)

        if p_i == 0:
            for l in range(nl):
                DMA("sp", sa_old[l], sa_in[l, 4:30], misc_sem(), [], [])

        for l in range(nl):
            VB = l * NVEC
            if p_i == 0:
                rg_load(l)
            if is_last_pass:
                for i in range(4):
                    nr = 128 if i < 3 else 96
                    nj = nr // 16
                    slot = i % 2
                    DMA("sp", ORW[0:nr, slot, :],
                        sa_in[l].rearrange("j s d -> (j s) d")[i * 128: i * 128 + nr, :],
                        misc_sem(), [], [("ORW", slot)])
                    for half in range(2):
                        b = bank()
                        for j in range(4):
                            c = half * 4 + j
                            TR(PS[:, b, j * 128: j * 128 + nr], ORW[0:nr, slot, c * 128:(c + 1) * 128],
                               IDF[0:nr, 0:nr], [("ORW", slot), ("IDF",)], [psk(b)])
                        for j in range(4):
                            c = half * 4 + j
                            ACT(US[:, c, :, i * 8: i * 8 + nj],
                                PS[:, b, j * 128: j * 128 + nr].rearrange("p (j s) -> p s j", s=NSEQ),
                                AF.Identity, [psk(b)], [("US", c)])
                        bfree(b)
                slot = 0
                DMA("sp", ORW[0:96, slot, :], sbc_in[l], misc_sem(), [], [("ORW", slot)])
                for half in range(2):
                    b = bank()
                    for j in range(4):
                        c = half * 4 + j
                        TR(PS[:, b, j * 128: j * 128 + 96], ORW[0:96, slot, c * 128:(c + 1) * 128],
                           IDF[0:96, 0:96], [("ORW", slot), ("IDF",)], [psk(b)])
                    for j in range(4):
                        c = half * 4 + j
                        ACT(BXS[:, c, :, 0:3], PS[:, b, j * 128: j * 128 + 48].rearrange("p (j s) -> p s j", s=NSEQ),
                            AF.Identity, [psk(b)], [("BXS", c)])
                        ACT(LRUS[:, c, :], PS[:, b, j * 128 + 48: j * 128 + 64],
                            AF.Identity, [psk(b)], [("LRUS", c)])
                        ACT(CINS[:, c, :, 0:2], PS[:, b, j * 128 + 64: j * 128 + 96].rearrange("p (j s) -> p s j", s=NSEQ),
                            AF.Identity, [psk(b)], [("CINS", c)])
                    bfree(b)

            MARK(('PRE', p_i, l if 'PRE' != 'Y' else -1))
            for (ti, st, nt, kind, off) in tiles:
                bs = hold_bank()
                for c in range(NCH):
                    i = c % 2
                    ACT(tb(i, nt), X[:, c, off:off + nt], AF.Square, [Xk(c)], [tbk(i)])
                    MM(PS[:, bs, 0:nt], ONESB[:], tb(i, nt), c == 0, c == NCH - 1, [tbk(i), ("ONESB",)], [psk(bs)])
                stats_rs(bs, nt, 0)
                bfree(bs)
                for c in range(NCH):
                    STT(H[:, c, off:off + nt], X[:, c, off:off + nt], vt(l, V_NPRE, c), tf(0, nt),
                        ALU.mult, ALU.mult, [Xk(c), tfk(0), ("VT",)], [Hk(c)])

            if p_i == 0 and l == 0:
                DBG("H", H[:], [128, NCH, TPB], BF16, [Hk(c) for c in range(NCH)])
            MARK(('A1', p_i, l if 'A1' != 'Y' else -1))
            def a1_unit(cs, utiles, par):
                q = cs[0] // 2
                items = [(c, t) for c in cs for t in utiles]
                rel = (cs[-1] % 2 == 1) and (utiles[-1] is tiles[-1])
                bv = {}
                bg = {}
                for c in cs:
                    bv[c] = proj_multi(p_i, l, "w_in", C_AV + q * GC, c % 2, H, Hk, utiles)
                    bg[c] = proj_multi(p_i, l, "w_in", C_AG + q * GC, c % 2, H, Hk, utiles)
                if rel:
                    ws_release(p_i, l, "w_in", C_AV + q * GC)
                    ws_release(p_i, l, "w_in", C_AG + q * GC)
                yield
                for j, (c, (ti, st, nt, kind, off)) in enumerate(items):
                    si = par * 2 + j
                    ACT(tf(si, nt), PS[:, bg[c][ti], 0:nt], AF.Sigmoid, [psk(bg[c][ti])], [tfk(si)],
                        bias=b_in(l, C_AG, c))
                for j, (c, (ti, st, nt, kind, off)) in enumerate(items):
                    si = par * 2 + j
                    bvb = bv[c][ti]
                    last_prompt = (kind == 'p' and ti == 3)
                    if kind == 'p':
                        ui = par * 2 + j
                        CP(TB[:, ui, 0:30], HA[:, l, c, :], [("HA", l, c)], [tbk(ui)], eng="act")
                        STT(TB[:, ui, 30:30 + nt], PS[:, bvb, 0:nt], b_in(l, C_AV, c), tf(si, nt),
                            ALU.add, ALU.mult, [psk(bvb), tfk(si), ("VT",)], [tbk(ui)])
                        if last_prompt:
                            STT(SOUT[:, c, 0:30], PS[:, bvb, nt - 30:nt], b_in(l, C_AV, c), tf(si, 30, nt - 30),
                                ALU.add, ALU.mult, [psk(bvb), tfk(si), ("VT",)], [("SOUT", c)])
                    else:
                        STT(US[:, c, :, 30:34], PS[:, bvb, 0:nt].rearrange("p (s t) -> p s t", t=4),
                            b_in(l, C_AV, c), tf(si, nt).rearrange("p (s t) -> p s t", t=4),
                            ALU.add, ALU.mult, [psk(bvb), tfk(si), ("VT",)], [("US", c)])
                        STT(SOUT[:, c, 36:100].rearrange("p (s t) -> p s t", t=4),
                            PS[:, bvb, 0:nt].rearrange("p (s t) -> p s t", t=4),
                            b_in(l, C_AV, c), tf(si, nt).rearrange("p (s t) -> p s t", t=4),
                            ALU.add, ALU.mult, [psk(bvb), tfk(si), ("VT",)], [("SOUT", c)])
                    bfree(bvb, bg[c][ti])
                yield
                for c in cs:
                    di = c % 2
                    if utiles[0] is tiles[0]:
                        if p_i == 0 and l == 0:
                            TT(DG[:, di], IDB[:].unsqueeze(1).to_broadcast([128, 31, 128]),
                               VT[:, c, VB + V_CAW: VB + V_CAW + 31].unsqueeze(2).to_broadcast([128, 31, 128]),
                               ALU.mult, [("IDB",), ("VT",)], [("DG", di)], eng=DG_ENG)
                            if len(passes) > 1:
                                DMA("sp", dgs[l, c], DG[:, di].rearrange("p k m -> p (k m)"), "dgst%d" % di,
                                    [("DG", di)], [("dgs", l, c)])
                        else:
                            DMA("sp", DG[:, di].rearrange("p k m -> p (k m)"), dgs[l, c], "dgld%d" % di,
                                [("dgs", l, c)], [("DG", di)])
                for j, (c, (ti, st, nt, kind, off)) in enumerate(items):
                    if kind == 'p' and ti != 3:
                        ui = par * 2 + j
                        CP(HA[:, l, c, :], TB[:, ui, nt:nt + 30], [tbk(ui)], [("HA", l, c)], eng="act")
                yield
                bc = {}
                for c in cs:
                    di = c % 2
                    its = [(j, it) for j, it in enumerate(items) if it[0] == c]
                    for j, (c_, (ti, st, nt, kind, off)) in its:
                        bc[(c, ti)] = bank()
                    for k in range(31):
                        for j, (c_, (ti, st, nt, kind, off)) in its:
                            if kind == 'p':
                                rhs = TB[:, par * 2 + j, k:k + nt]
                                rk = tbk(par * 2 + j)
                            else:
                                rhs = US[:, c, :, k:k + 4]
                                rk = ("US", c)
                            MM(PS[:, bc[(c, ti)], 0:nt], DG[:, di, k, :], rhs, k == 0, k == 30,
                               [("DG", di), rk], [psk(bc[(c, ti)])])
                yield
                for j, (c, (ti, st, nt, kind, off)) in enumerate(items):
                    if is_last_pass:
                        TS(Rb[:, c, off:off + nt], PS[:, bc[(c, ti)], 0:nt], vt(l, V_CAB, c), None, ALU.add, None,
                           [psk(bc[(c, ti)]), ("VT",)], [Rk(c)])
                    else:
                        ACT(Rb[:, c, off:off + nt], PS[:, bc[(c, ti)], 0:nt], AF.Identity, [psk(bc[(c, ti)])], [Rk(c)],
                            bias=vt(l, V_CAB, c))
                    bfree(bc[(c, ti)])

            def b_unit(c, tile, par):
                ti, st, nt, kind, off = tile
                last_prompt = (kind == 'p' and ti == 3)
                q, cc = c // 2, c % 2
                bi = 4 + par * 5
                xi = bi + 1
                ri = bi + 2
                ai = bi + 3
                zi = bi + 4
                xbb = 4 + par
                bx = proj(p_i, l, "w_in", C_BX + q * GC, cc, H, Hk, off, nt)
                if tile is tiles[-1] and cc == 1:
                    ws_release(p_i, l, "w_in", C_BX + q * GC)
                yield
                if kind == 'p':
                    CP(TF[:, bi, 0:3], HB[:, l, c, :], [("HB", l, c)], [tfk(bi)], eng="act")
                    if is_last_pass:
                        TS(TF[:, bi, 3:3 + nt], PS[:, bx, 0:nt], b_in(l, C_BX, c), None, ALU.add, None,
                           [psk(bx), ("VT",)], [tfk(bi)])
                    else:
                        ACT(TF[:, bi, 3:3 + nt], PS[:, bx, 0:nt], AF.Identity, [psk(bx), ("VT",)], [tfk(bi)],
                            bias=b_in(l, C_BX, c))
                    taps = [TF[:, bi, k:k + nt] for k in range(4)]
                    xb = tf(xi, nt)
                    rk = tfk(bi)
                else:
                    ACT(BXS[:, c, :, 3:7], PS[:, bx, 0:nt].rearrange("p (s t) -> p s t", t=4), AF.Identity,
                        [psk(bx), ("VT",)], [("BXS", c)], bias=b_in(l, C_BX, c))
                    taps = [BXS[:, c, :, k:k + 4] for k in range(4)]
                    xb = tf(xi, nt).rearrange("p (s t) -> p s t", t=4)
                    rk = ("BXS", c)
                bfree(bx)
                TS(xb, taps[3], vt(l, V_CBW + 3, c), vt(l, V_CBB, c), ALU.mult, ALU.add,
                   [rk, ("VT",)], [tfk(xi)])
                for k in (2, 1, 0):
                    STT(xb, taps[k], vt(l, V_CBW + k, c), xb, ALU.mult, ALU.add,
                        [rk, tfk(xi), ("VT",)], [tfk(xi)])
                yield
                CP(tb(xbb, nt), tf(xi, nt), [tfk(xi)], [tbk(xbb)], eng=("dve" if is_last_pass else "act"))
                if kind == 'p':
                    if last_prompt:
                        CP(SOUT[:, c, 30:33], TF[:, bi, nt:nt + 3], [tfk(bi)], [("SOUT", c)], eng="act")
                    else:
                        CP(HB[:, l, c, :], TF[:, bi, nt:nt + 3], [tfk(bi)], [("HB", l, c)], eng="act")
                else:
                    CP(SOUT[:, c, 100:148].rearrange("p (s j) -> p s j", j=3), BXS[:, c, :, 4:7],
                       [("BXS", c)], [("SOUT", c)], eng="act")
                yield
                br = bank()
                MM(PS[:, br, 0:nt], RG[:, l, c, 0, :], tb(xbb, nt), True, True,
                   [("RG", l, 0, 0), ("RG", l, 0, 1), tbk(xbb)], [psk(br)])
                bi_ = bank()
                MM(PS[:, bi_, 0:nt], RG[:, l, c, 1, :], tb(xbb, nt), True, True,
                   [("RG", l, 1, 0), ("RG", l, 1, 1), tbk(xbb)], [psk(bi_)])
                bz = proj(p_i, l, "w_in", C_BZ + q * GC, cc, H, Hk, off, nt)
                if tile is tiles[-1] and cc == 1:
                    ws_release(p_i, l, "w_in", C_BZ + q * GC)
                yield
                ACT(tf(ri, nt), PS[:, br, 0:nt], AF.Sigmoid, [psk(br), ("VT",)], [tfk(ri)], bias=vt(l, V_RBA, c))
                ACT(tf(bi, nt), PS[:, bi_, 0:nt], AF.Sigmoid, [psk(bi_), ("VT",)], [tfk(bi)], bias=vt(l, V_RBX, c))
                ACT(tf(zi, nt), PS[:, bz, 0:nt], AF.Sigmoid, [psk(bz), ("VT",)], [tfk(zi)], bias=b_in(l, C_BZ, c))
                ACT(tf(ai, nt), tf(ri, nt), AF.Exp, [tfk(ri), ("CL",)], [tfk(ai)], scale=CL[:, l, c:c + 1])
                ACT(tf(ri, nt), tf(ri, nt), AF.Exp, [tfk(ri), ("CL",)], [tfk(ri)], scale=CL2[:, l, c:c + 1])
                ACT(tf(ri, nt), tf(ri, nt), AF.Ln, [tfk(ri)], [tfk(ri)], bias=1.0, scale=-1.0)
                ACT(tf(ri, nt), tf(ri, nt), AF.Exp, [tfk(ri)], [tfk(ri)], scale=0.5)
                bfree(br, bi_)
                TT(tf(bi, nt), tf(bi, nt), tf(xi, nt), ALU.mult, [tfk(bi), tfk(xi)], [tfk(bi)])
                TT(tf(ri, nt), tf(ri, nt), tf(bi, nt), ALU.mult, [tfk(ri), tfk(bi)], [tfk(ri)])
                if kind == 'p':
                    T.op("dve", lambda e, o=tf(xi, nt), a=tf(ai, nt), u=tf(ri, nt), h0=HL[:, l, c:c + 1]:
                         e.tensor_tensor_scan(o, a, u, h0, ALU.mult, ALU.add),
                         [tfk(ai), tfk(ri), ("HL", l, c)], [tfk(xi)])
                    if last_prompt:
                        CP(SOUT[:, c, 33:34], TF[:, xi, nt - 1:nt], [tfk(xi)], [("SOUT", c)], eng="dve")
                    else:
                        CP(HL[:, l, c:c + 1], TF[:, xi, nt - 1:nt], [tfk(xi)], [("HL", l, c)], eng="dve")
                    hs_keys = [tfk(xi)]
                else:
                    for s_ in range(NSEQ):
                        T.op("dve", lambda e, o=TF[:, xi, s_ * 4:s_ * 4 + 4], a=TF[:, ai, s_ * 4:s_ * 4 + 4],
                             u=TF[:, ri, s_ * 4:s_ * 4 + 4], h0=LRUS[:, c, s_:s_ + 1]:
                             e.tensor_tensor_scan(o, a, u, h0, ALU.mult, ALU.add),
                             [tfk(ai), tfk(ri), ("LRUS", c)], [("sub", xi, s_)])
                    hs_keys = [tfk(xi)] + [("sub", xi, s_) for s_ in range(NSEQ)]
                    CP(SOUT[:, c, 148:164], tf(xi, nt).rearrange("p (s t) -> p s t", t=4)[:, :, 3],
                       hs_keys, [("SOUT", c)], eng="dve")
                STT(tf(xi, nt), PS[:, bz, 0:nt], b_in(l, C_BZ, c), tf(xi, nt), ALU.add, ALU.mult,
                    hs_keys + [psk(bz), ("VT",)], [tfk(xi)])
                bfree(bz)
                TT(Rb[:, 8 + c, off:off + nt], tf(xi, nt), tf(zi, nt), ALU.mult, [tfk(xi), tfk(zi)], [Rk(8 + c)])

            def unit_groups(q):
                if len(tiles) == 1:
                    return [((2 * q, 2 * q + 1), tiles)]
                return [((2 * q,), tiles), ((2 * q + 1,), tiles)]

            units = []
            nA = 0
            nB = 0
            for q in range(4):
                for (cs_, ut_) in unit_groups(q):
                    units.append(a1_unit(cs_, ut_, nA % 2))
                    nA += 1
                for c in (2 * q, 2 * q + 1):
                    for tile in tiles:
                        units.append(b_unit(c, tile, nB % 3))
                        nB += 1
            run_pipeline(units)

            if p_i == 0 and l == 0:
                DBG("V", R32[:], [128, NCH * TPB], F32, [Rk(c) for c in range(16)])
            MARK(('A2', p_i, l if 'A2' != 'Y' else -1))
            ln_slots = {}
            for tix, (ti, st, nt, kind, off) in enumerate(tiles):
                b1 = hold_bank()
                b2 = hold_bank()
                for c in range(NCH):
                    i = c % 2
                    ACT(tb(i, nt), Rb[:, c, off:off + nt], AF.Square, [Rk(c)], [tbk(i)])
                    MM(PS[:, b1, 0:nt], ONESB[:], Rb[:, c, off:off + nt], c == 0, c == NCH - 1,
                       [Rk(c), ("ONESB",)], [psk(b1)])
                    MM(PS[:, b2, 0:nt], ONESB[:], tb(i, nt), c == 0, c == NCH - 1,
                       [tbk(i), ("ONESB",)], [psk(b2)])
                mu_i = 10 + 2 * tix
                rs_i = 11 + 2 * tix
                ln_slots[ti] = (mu_i, rs_i)
                T.op("act", lambda e, o=tf(mu_i, nt), i_=PS[:, b1, 0:nt]: e.mul(o, i_, 1.0 / D),
                     [psk(b1)], [tfk(mu_i)])
                TT(tf(rs_i, nt), tf(mu_i, nt), tf(mu_i, nt), ALU.mult, [tfk(mu_i)], [tfk(rs_i)])
                STT(tf(rs_i, nt), PS[:, b2, 0:nt], 1.0 / D, tf(rs_i, nt), ALU.mult, ALU.subtract,
                    [psk(b2), tfk(rs_i)], [tfk(rs_i)])
                bfree(b1, b2)
                ACT(tf(rs_i, nt), tf(rs_i, nt), AF.Ln, [tfk(rs_i)], [tfk(rs_i)], bias=EPS)
                ACT(tf(rs_i, nt), tf(rs_i, nt), AF.Exp, [tfk(rs_i)], [tfk(rs_i)], scale=-0.5)

            MARK(('A3', p_i, l if 'A3' != 'Y' else -1))
            def a3_unit(cs, utiles, par):
                q = cs[0] // 2
                items = [(c, t) for c in cs for t in utiles]
                rel = (cs[-1] % 2 == 1) and (utiles[-1] is tiles[-1])
                bz = {}
                for c in cs:
                    bz[c] = proj_multi(p_i, l, "w_in", C_AZ + q * GC, c % 2, H, Hk, utiles)
                if rel:
                    ws_release(p_i, l, "w_in", C_AZ + q * GC)
                yield
                for j, (c, (ti, st, nt, kind, off)) in enumerate(items):
                    mu_i, rs_i = ln_slots[ti]
                    ni = par * 2 + j
                    TT(tf(ni, nt), Rb[:, c, off:off + nt], tf(mu_i, nt), ALU.subtract, [Rk(c), tfk(mu_i)], [tfk(ni)])
                    TT(tf(ni, nt), tf(ni, nt), tf(rs_i, nt), ALU.mult, [tfk(ni), tfk(rs_i)], [tfk(ni)])
                for j, (c, (ti, st, nt, kind, off)) in enumerate(items):
                    ni = par * 2 + j
                    zi = 4 + par * 2 + j
                    ACT(tf(ni, nt), tf(ni, nt), AF.Silu, [tfk(ni), ("VT",)], [tfk(ni)],
                        bias=vt(l, V_LNB, c), scale=vt(l, V_LNG, c))
                    ACT(tf(zi, nt), PS[:, bz[c][ti], 0:nt], AF.Silu, [psk(bz[c][ti]), ("VT",)], [tfk(zi)],
                        bias=b_in(l, C_AZ, c))
                    bfree(bz[c][ti])
                for j, (c, (ti, st, nt, kind, off)) in enumerate(items):
                    ni = par * 2 + j
                    zi = 4 + par * 2 + j
                    TT(Rb[:, c, off:off + nt], tf(ni, nt), tf(zi, nt), ALU.mult, [tfk(ni), tfk(zi)], [Rk(c)])

            units = []
            n = 0
            for q in range(4):
                for (cs_, ut_) in unit_groups(q):
                    units.append(a3_unit(cs_, ut_, n % 2))
                    n += 1
            run_pipeline(units)

            if p_i == 0 and l == 0:
                DBG("MU", TF[:, 10:12, :], [128, 2, 516], F32, [tfk(10), tfk(11)])
                DBG("PA", R32[:], [128, NCH * TPB], F32, [Rk(c) for c in range(16)])
            MARK(('C', p_i, l if 'C' != 'Y' else -1))
            def c_unit(cs, utiles, par):
                q = cs[0] // 2
                items = [(c, t) for c in cs for t in utiles]
                rel = (cs[-1] % 2 == 1) and (utiles[-1] is tiles[-1])
                bcc = {}
                bcx = {}
                for c in cs:
                    bcc[c] = proj_multi(p_i, l, "w_in", C_CC + q * GC, c % 2, H, Hk, utiles)
                    bcx[c] = proj_multi(p_i, l, "w_in", C_CX + q * GC, c % 2, H, Hk, utiles)
                if rel:
                    ws_release(p_i, l, "w_in", C_CC + q * GC)
                    ws_release(p_i, l, "w_in", C_CX + q * GC)
                yield
                for j, (c, (ti, st, nt, kind, off)) in enumerate(items):
                    last_prompt = (kind == 'p' and ti == 3)
                    ci = par * 2 + j
                    ni = 4 + par * 2 + j
                    xi = 8 + par * 2 + j
                    b1, b2 = bcc[c][ti], bcx[c][ti]
                    ACT(tf(ci, nt), PS[:, b1, 0:nt], AF.Identity, [psk(b1), ("VT",)], [tfk(ci)],
                        bias=b_in(l, C_CC, c))
                    if kind == 'p':
                        CP(TF[:, ni, 0:2], HC[:, l, c, :], [("HC", l, c)], [tfk(ni)], eng="act")
                        STT(TF[:, ni, 2:2 + nt], PS[:, b2, 0:nt], b_in(l, C_CX, c), tf(ci, nt),
                            ALU.add, ALU.mult, [psk(b2), tfk(ci), ("VT",)], [tfk(ni)])
                        if last_prompt:
                            CP(SOUT[:, c, 34:36], TF[:, ni, nt:nt + 2], [tfk(ni)], [("SOUT", c)], eng="dve")
                        else:
                            CP(HC[:, l, c, :], TF[:, ni, nt:nt + 2], [tfk(ni)], [("HC", l, c)], eng="dve")
                        taps = [TF[:, ni, k:k + nt] for k in range(3)]
                        cx = tf(xi, nt)
                        rk = tfk(ni)
                    else:
                        STT(CINS[:, c, :, 2:6], PS[:, b2, 0:nt].rearrange("p (s t) -> p s t", t=4),
                            b_in(l, C_CX, c), tf(ci, nt).rearrange("p (s t) -> p s t", t=4),
                            ALU.add, ALU.mult, [psk(b2), tfk(ci), ("VT",)], [("CINS", c)])
                        CP(SOUT[:, c, 164:196].rearrange("p (s j) -> p s j", j=2), CINS[:, c, :, 4:6],
                           [("CINS", c)], [("SOUT", c)], eng="dve")
                        taps = [CINS[:, c, :, k:k + 4] for k in range(3)]
                        cx = tf(xi, nt).rearrange("p (s t) -> p s t", t=4)
                        rk = ("CINS", c)
                    TS(cx, taps[2], vt(l, V_CCW + 2, c), None, ALU.mult, None, [rk, ("VT",)], [tfk(xi)])
                    for k in (1, 0):
                        STT(cx, taps[k], vt(l, V_CCW + k, c), cx, ALU.mult, ALU.add,
                            [rk, tfk(xi), ("VT",)], [tfk(xi)])
                    bfree(b1, b2)
                yield
                bcb = {}
                bcz = {}
                for c in cs:
                    bcb[c] = proj_multi(p_i, l, "w_in", C_CB + q * GC, c % 2, H, Hk, utiles)
                    bcz[c] = proj_multi(p_i, l, "w_in", C_CZ + q * GC, c % 2, H, Hk, utiles)
                if rel:
                    ws_release(p_i, l, "w_in", C_CB + q * GC)
                    ws_release(p_i, l, "w_in", C_CZ + q * GC)
                yield
                for j, (c, (ti, st, nt, kind, off)) in enumerate(items):
                    ci = par * 2 + j
                    xi = 8 + par * 2 + j
                    b1, b2 = bcb[c][ti], bcz[c][ti]
                    STT(tf(xi, nt), PS[:, b1, 0:nt], b_in(l, C_CB, c), tf(xi, nt), ALU.add, ALU.mult,
                        [psk(b1), tfk(xi), ("VT",)], [tfk(xi)])
                    ACT(tf(ci, nt), PS[:, b2, 0:nt], AF.Silu, [psk(b2), ("VT",), tfk(ci)], [tfk(ci)],
                        bias=b_in(l, C_CZ, c))
                    TT(PC[:, c, off:off + nt], tf(xi, nt), tf(ci, nt), ALU.mult, [tfk(xi), tfk(ci)], [PCk(c)])
                    bfree(b1, b2)

            units = []
            n = 0
            for q in range(4):
                for (cs_, ut_) in unit_groups(q):
                    units.append(c_unit(cs_, ut_, n % 2))
                    n += 1
            run_pipeline(units)

            MARK(('SO', p_i, l if 'SO' != 'Y' else -1))
            if is_last_pass:
                for c in range(NCH):
                    for (c0, ncol, slot) in ((0, 128, 0), (128, NSOUT - 128, 1)):
                        pass
                for (c0, ncol, slot) in ((0, 128, 0), (128, NSOUT - 128, 1)):
                    for half in range(2):
                        b = bank()
                        for j in range(4):
                            c = half * 4 + j
                            TR(PS[0:ncol, b, j * 128:(j + 1) * 128], SOUT[:, c, c0:c0 + ncol], IDF[:],
                               [("SOUT", c), ("IDF",)], [psk(b)])
                        ACT(ORW[0:ncol, slot, half * 512:(half + 1) * 512], PS[0:ncol, b, :], AF.Identity,
                            [psk(b)], [("ORW", slot)])
                        bfree(b)
                    DMA("sp", st_out[l, c0:c0 + ncol, :], ORW[0:ncol, slot, :], misc_sem(), [("ORW", slot)], [])

            if p_i == 0 and l == 0:
                DBG("PB", R32[:], [128, NCH * TPB], F32, [Rk(c) for c in range(16)])
                DBG("PC", PC[:], [128, NCH, TPB], BF16, [PCk(c) for c in range(NCH)])
            MARK(('MERGE', p_i, l if 'MERGE' != 'Y' else -1))
            def m_unit(cs, utiles, par, jb):
                q = cs[0] // 2
                items = [(c, t) for c in cs for t in utiles]
                rel = (cs[-1] % 2 == 1) and (utiles[-1] is tiles[-1])
                srcs = (("w_a_out", lambda k: Rk(k), 0), ("w_b_out", lambda k: Rk(8 + k), 8), ("w_c_out", PCk, None))
                name, keyf, rbase = srcs[jb]
                by = {}
                bgt = {}
                slot = ws_get(p_i, l, name, q * GC)
                for c in cs:
                    by[c] = {t[0]: bank() for t in utiles}
                    for k in range(NCH):
                        for (ti, st, nt, kind, off) in utiles:
                            rhs = Rb[:, rbase + k, off:off + nt] if rbase is not None else PC[:, k, off:off + nt]
                            MM(PS[:, by[c][ti], 0:nt], RING[:, slot, k, (c % 2) * 128:(c % 2 + 1) * 128], rhs,
                               k == 0, k == NCH - 1, [("wr", slot), keyf(k)], [psk(by[c][ti])])
                    bgt[c] = proj_multi(p_i, l, "w_in", C_G + jb * 1024 + q * GC, c % 2, H, Hk, utiles)
                if rel:
                    ws_release(p_i, l, name, q * GC)
                    ws_release(p_i, l, "w_in", C_G + jb * 1024 + q * GC)
                yield
                for j, (c, (ti, st, nt, kind, off)) in enumerate(items):
                    gi = (jb % 2) * 2 + j
                    t2 = 8 + (jb % 2) * 2 + j
                    mt = 4 + par * 2 + j
                    b1, b2 = by[c][ti], bgt[c][ti]
                    ACT(tf(gi, nt), PS[:, b2, 0:nt], AF.Sigmoid, [psk(b2), ("VT",)], [tfk(gi)],
                        bias=b_in(l, C_G + jb * 1024, c))
                    if jb == 0:
                        TT(tf(mt, nt), tf(gi, nt), PS[:, b1, 0:nt], ALU.mult, [tfk(gi), psk(b1)], [tfk(mt)])
                    else:
                        TT(tf(t2, nt), tf(gi, nt), PS[:, b1, 0:nt], ALU.mult, [tfk(gi), psk(b1)], [tfk(t2)])
                        if jb == 1:
                            TT(tf(mt, nt), tf(mt, nt), tf(t2, nt), ALU.add, [tfk(mt), tfk(t2)], [tfk(mt)])
                        else:
                            TT(M[:, c, off:off + nt], tf(mt, nt), tf(t2, nt), ALU.add, [tfk(mt), tfk(t2)], [Mk(c)])
                    bfree(b1, b2)

            def diag_unit(ln, c):
                di = c % 2
                TT(DG[:, di], IDB[:].unsqueeze(1).to_broadcast([128, 31, 128]),
                   VT[:, c, ln * NVEC + V_CAW: ln * NVEC + V_CAW + 31].unsqueeze(2).to_broadcast([128, 31, 128]),
                   ALU.mult, [("IDB",), ("VT",)], [("DG", di)], eng=DG_ENG)
                DMA("sp", dgs[ln, c], DG[:, di].rearrange("p k m -> p (k m)"), "dgst%d" % di,
                    [("DG", di)], [("dgs", ln, c)])
                return
                yield

            units = []
            n = 0
            ndg = 0
            for q in range(4):
                for (cs_, ut_) in unit_groups(q):
                    for jb in range(3):
                        units.append(m_unit(cs_, ut_, n % 2, jb))
                        if p_i == 0 and l + 1 < nl and ndg < NCH:
                            units.append(diag_unit(l + 1, ndg))
                            ndg += 1
                    n += 1
            run_pipeline(units)

            if p_i == 0 and l == 0:
                DBG("M", M[:], [128, NCH, TPB], BF16, [Mk(c) for c in range(NCH)])
            MARK(('WO', p_i, l if 'WO' != 'Y' else -1))
            ss_bank = {}
            for (ti, st, nt, kind, off) in tiles:
                ss_bank[ti] = hold_bank()

            def wo_unit(cs, utiles, par):
                q = cs[0] // 2
                items = [(c, t) for c in cs for t in utiles]
                rel = (cs[-1] % 2 == 1) and (utiles[-1] is tiles[-1])
                bo = {}
                for c in cs:
                    bo[c] = proj_multi(p_i, l, "w_o", q * GC, c % 2, M, Mk, utiles)
                if rel:
                    ws_release(p_i, l, "w_o", q * GC)
                yield
                for j, (c, (ti, st, nt, kind, off)) in enumerate(items):
                    si = par * 2 + j
                    b1 = bo[c][ti]
                    ACT(O[:, c, off:off + nt], PS[:, b1, 0:nt], AF.Identity, [psk(b1)],
                        [Rk(2 * c), Rk(2 * c + 1)])
                    ACT(tb(si, nt), PS[:, b1, 0:nt], AF.Square, [psk(b1)], [tbk(si)])
                    MM(PS[:, ss_bank[ti], 0:nt], ONESB[:], tb(si, nt), c == 0, c == NCH - 1,
                       [tbk(si), ("ONESB",)], [psk(ss_bank[ti])])
                    bfree(b1)

            units = []
            n = 0
            for q in range(4):
                for (cs_, ut_) in unit_groups(q):
                    units.append(wo_unit(cs_, ut_, n % 2))
                    n += 1
            run_pipeline(units)
            for tix, (ti, st, nt, kind, off) in enumerate(tiles):
                rs_i = 12 + tix
                stats_rs(ss_bank[ti], nt, rs_i)
                bfree(ss_bank[ti])
                for c in range(NCH):
                    t_i = c % 4
                    TT(tf(t_i, nt), O[:, c, off:off + nt], tf(rs_i, nt), ALU.mult,
                       [Rk(2 * c), Rk(2 * c + 1), tfk(rs_i)], [tfk(t_i)])
                    STT(X[:, c, off:off + nt], tf(t_i, nt), vt(l, V_NPOST, c), X[:, c, off:off + nt],
                        ALU.mult, ALU.add, [tfk(t_i), Xk(c), ("VT",)], [Xk(c)])

        if p_i == 0:
            DBG("O", R32[:], [128, NCH * TPB], F32, [Rk(c) for c in range(16)])
            DBG("X", X[:], [128, NCH, TPB], F32, [Xk(c) for c in range(NCH)])
        MARK(('Y', p_i, l if 'Y' != 'Y' else -1))
        for (ti, st, nt, kind, off) in tiles:
            for r0 in range(0, nt, 128):
                nr = min(128, nt - r0)
                slot = (r0 // 128) % 2
                for half in range(2):
                    b = bank()
                    for j in range(4):
                        c = half * 4 + j
                        TR(PS[0:nr, b, j * 128:(j + 1) * 128], X[:, c, off + r0: off + r0 + nr], IDF[:],
                           [Xk(c), ("IDF",)], [psk(b)])
                    ACT(ORW[0:nr, slot, half * 512:(half + 1) * 512], PS[0:nr, b, :], AF.Identity,
                        [psk(b)], [("ORW", slot)])
                    bfree(b)
                DMA("sp", yout[st + r0: st + r0 + nr, :], ORW[0:nr, slot, :], misc_sem(), [("ORW", slot)], [])

    for e in Tracker.ENGS:
        v = 0
        for I in T.lists[e]:
            if I.inc and not I.is_dma:
                v += 1
            I.val = v

    def emit(engname, eh):
        waited = {}
        for I in T.lists[engname]:
            req = {}
            for d in I.deps:
                if d.is_dma:
                    key = ("dma", d.sem)
                    val = d.semval
                else:
                    if d.eng == "pe" and engname == "pe" and not I.is_dma:
                        continue
                    key = ("eng", d.eng)
                    val = d.val
                if val > req.get(key, 0):
                    req[key] = val
            for key, val in req.items():
                if waited.get(key, 0) >= val:
                    continue
                eh.wait_ge(sems[key], val)
                waited[key] = val
            ins = I.fn(eh)
            if I.is_dma:
                ins.then_inc(sems[("dma", I.sem)], 16)
            elif I.inc:
                ins.then_inc(sems[("eng", engname)], 1)
        if engname == "sp":
            for name, lst in T.dma_sems.items():
                eh.wait_ge(sems[("dma", name)], 16 * len(lst))

    with es:
        with nc.Block() as block:
            @block.tensor
            def _(e):
                emit("pe", e)

            @block.scalar
            def _(e):
                emit("act", e)

            @block.vector
            def _(e):
                emit("dve", e)

            @block.gpsimd
            def _(e):
                emit("pool", e)

            @block.sync
            def _(e):
                emit("sp", e)
    stats = {e: len(T.lists[e]) for e in Tracker.ENGS}
    stats['marks'] = marks
    return nc, stats


def pack_inputs(inp):
    f = lambda a: np.ascontiguousarray(np.asarray(a, dtype=np.float32))
    vec_rows = []
    for l in range(NLAYERS):
        rows = [inp["norm_pre"][l][None], inp["norm_post"][l][None], np.asarray(inp["b_in"][l]).reshape(12, D),
                inp["conv_a_w"][l], inp["conv_a_b"][l][None], inp["ln_a_g"][l][None], inp["ln_a_b"][l][None],
                inp["conv_b_w"][l], inp["conv_b_b"][l][None], inp["rg_b_a"][l][None], inp["rg_b_x"][l][None],
                inp["rg_lambda"][l][None], inp["conv_c_w"][l]]
        vec_rows.append(np.concatenate([np.asarray(r, dtype=np.float32) for r in rows], axis=0))
    vecs = np.zeros((256, D), np.float32)
    vv = np.concatenate(vec_rows, axis=0)
    vecs[:vv.shape[0]] = vv
    shared = {
        "vecs": vecs, "ident": np.eye(128, dtype=np.float32),
        "w_in": f(inp["w_in"]), "w_a_out": f(inp["w_a_out"]), "w_b_out": f(inp["w_b_out"]),
        "w_c_out": f(inp["w_c_out"]), "w_o": f(inp["w_o"]),
        "rg_w_a": f(inp["rg_w_a"]), "rg_w_x": f(inp["rg_w_x"]),
    }
    xp = np.asarray(inp["x_prompt"], np.float32)
    xs = np.asarray(inp["x_sample"], np.float32)
    sa = np.asarray(inp["state_conv_a"], np.float32)
    sbb = np.asarray(inp["state_conv_b"], np.float32)
    sl = np.asarray(inp["state_lru"], np.float32)
    sc = np.asarray(inp["state_conv_c"], np.float32)
    maps = []
    for b in range(NCORES):
        s0, s1 = b * NSEQ, (b + 1) * NSEQ
        xin = np.concatenate([xp[b], xs[s0:s1].reshape(NSEQ * 4, D)], axis=0)
        sa_in = np.ascontiguousarray(sa[:, s0:s1].transpose(0, 2, 1, 3))
        sbc = np.concatenate([sbb[:, s0:s1].transpose(0, 2, 1, 3).reshape(NLAYERS, 48, D),
                              sl[:, s0:s1],
                              sc[:, s0:s1].transpose(0, 2, 1, 3).reshape(NLAYERS, 32, D)], axis=1)
        m = dict(shared)
        m["xin"] = np.ascontiguousarray(xin)
        m["sa_in"] = sa_in
        m["sbc_in"] = np.ascontiguousarray(sbc)
        maps.append(m)
    return maps


def unpack_outputs(results, nl=NLAYERS):
    y_p = np.zeros((NCORES, 2048, D), np.float32)
    y_s = np.zeros((NCORES * NSEQ, 4, D), np.float32)
    pa = np.zeros((NLAYERS, NCORES, 30, D), np.float32)
    pb = np.zeros((NLAYERS, NCORES, 3, D), np.float32)
    ph = np.zeros((NLAYERS, NCORES, D), np.float32)
    pc = np.zeros((NLAYERS, NCORES, 2, D), np.float32)
    sa = np.zeros((NLAYERS, NCORES * NSEQ, 30, D), np.float32)
    sb_ = np.zeros((NLAYERS, NCORES * NSEQ, 3, D), np.float32)
    sh = np.zeros((NLAYERS, NCORES * NSEQ, D), np.float32)
    sc = np.zeros((NLAYERS, NCORES * NSEQ, 2, D), np.float32)
    for b in range(NCORES):
        r = results[b]
        s0, s1 = b * NSEQ, (b + 1) * NSEQ
        y = np.asarray(r["yout"])
        y_p[b] = y[:2048]
        y_s[s0:s1] = y[2048:].reshape(NSEQ, 4, D)
        st = np.asarray(r["st_out"])
        old = np.asarray(r["sa_old"])
        pa[:, b] = st[:, 0:30]
        pb[:, b] = st[:, 30:33]
        ph[:, b] = st[:, 33]
        pc[:, b] = st[:, 34:36]
        sa[:, s0:s1, 0:26] = old.transpose(0, 2, 1, 3)
        sa[:, s0:s1, 26:30] = st[:, 36:100].reshape(NLAYERS, NSEQ, 4, D)
        sb_[:, s0:s1] = st[:, 100:148].reshape(NLAYERS, NSEQ, 3, D)
        sh[:, s0:s1] = st[:, 148:164]
        sc[:, s0:s1] = st[:, 164:196].reshape(NLAYERS, NSEQ, 2, D)
    return (y_p, y_s, pa, pb, ph, pc, sa, sb_, sh, sc)


def kernel(**inputs):
    maps = pack_inputs(inputs)
    nc, _ = build_program()
    res = run_bass_kernel_spmd(nc, maps, core_ids=list(range(NCORES)))
    return unpack_outputs(res.results)
```

```python
import contextlib
import numpy as np
import concourse.bass as bass
import concourse.mybir as mybir
from concourse.bass_utils import run_bass_kernel_spmd

F32 = mybir.dt.float32
BF16 = mybir.dt.bfloat16
AF = mybir.ActivationFunctionType
ALU = mybir.AluOpType

NCORES = 8
D = 1024
NCH = 8
IN_COLS = 12288
NSEQ = 16
EPS = 1e-6
NLAYERS = 4
TILES = [(0, 512, 'p'), (512, 512, 'p'), (1024, 512, 'p'), (1536, 512, 'p'), (2048, 64, 's')]
PASSES = [[0], [1], [2], [3, 4]]
TPB = 576
T_ALL = 2112
NVEC = 59
NSOUT = 196
NRING = 6
DG_ENG = "dve"
GC = 256

V_NPRE, V_NPOST, V_BIN, V_CAW, V_CAB, V_LNG, V_LNB, V_CBW, V_CBB, V_RBA, V_RBX, V_LAM, V_CCW = \
    0, 1, 2, 14, 45, 46, 47, 48, 52, 53, 54, 55, 56
C_AV, C_AG, C_AZ, C_BX, C_BZ, C_CB, C_CC, C_CX, C_CZ, C_G = \
    0, 1024, 2048, 3072, 4096, 5120, 6144, 7168, 8192, 9216


class Ins:
    __slots__ = ("eng", "fn", "deps", "idx", "inc", "val", "is_dma", "sem", "semval")


class Tracker:
    ENGS = ("pe", "act", "dve", "pool", "sp")

    def __init__(self):
        self.lists = {e: [] for e in self.ENGS}
        self.lastw = {}
        self.readers = {}
        self.dma_sems = {}

    def op(self, eng, fn, reads=(), writes=(), dma_sem=None):
        I = Ins()
        I.eng = eng
        I.fn = fn
        I.idx = len(self.lists[eng])
        I.inc = False
        I.val = 0
        I.is_dma = dma_sem is not None
        I.sem = dma_sem
        I.semval = 0
        deps = set()
        for r in reads:
            w = self.lastw.get(r)
            if w is not None:
                deps.add(w)
        for k in writes:
            w = self.lastw.get(k)
            if w is not None:
                deps.add(w)
            rd = self.readers.get(k)
            if rd:
                deps.update(rd.values())
        if dma_sem is not None:
            lst = self.dma_sems.setdefault(dma_sem, [])
            if lst:
                deps.add(lst[-1])
            I.semval = 16 * (len(lst) + 1)
            lst.append(I)
        deps.discard(I)
        I.deps = deps
        for d in deps:
            if not d.is_dma:
                d.inc = True
        for r in reads:
            rd = self.readers.setdefault(r, {})
            rd[("d", id(I)) if I.is_dma else eng] = I
        for k in writes:
            self.lastw[k] = I
            self.readers[k] = {}
        self.lists[eng].append(I)
        return I


def build_program(nl=NLAYERS, passes=PASSES, debug=False):
    nc = bass.Bass("TRN2", target_bir_lowering=False)
    T = Tracker()
    dbg_list = []
    marks = []

    def MARK(label):
        marks.append((label, len(T.lists['pe']), len(T.lists['act']), len(T.lists['dve'])))

    def DBG(name, ap, shape, dt, keys):
        if not debug:
            return
        dd = nc.dram_tensor("dbg_" + name, list(shape), dt, kind="ExternalOutput").ap()
        dbg_list.append(name)
        T.op("sp", lambda e: e.dma_start(out=dd, in_=ap), keys, [], dma_sem=dsem("dbg_" + name))

    def din(name, shape):
        return nc.dram_tensor(name, list(shape), F32, kind="ExternalInput").ap()

    def dout(name, shape):
        return nc.dram_tensor(name, list(shape), F32, kind="ExternalOutput").ap()

    xin = din("xin", [T_ALL, D])
    sa_in = din("sa_in", [NLAYERS, 30, NSEQ, D])
    sbc_in = din("sbc_in", [NLAYERS, 96, D])
    vecs = din("vecs", [256, D])
    ident_d = din("ident", [128, 128])
    w_in = din("w_in", [NLAYERS, D, IN_COLS])
    w_a_out = din("w_a_out", [NLAYERS, D, D])
    w_b_out = din("w_b_out", [NLAYERS, D, D])
    w_c_out = din("w_c_out", [NLAYERS, D, D])
    w_o = din("w_o", [NLAYERS, D, D])
    rg_w_a = din("rg_w_a", [NLAYERS, 16, 64, 64])
    rg_w_x = din("rg_w_x", [NLAYERS, 16, 64, 64])
    yout = dout("yout", [T_ALL, D])
    st_out = dout("st_out", [NLAYERS, NSOUT, D])
    sa_old = dout("sa_old", [NLAYERS, 26, NSEQ, D])
    dgs = nc.dram_tensor("dgs", [NLAYERS, NCH, 128, 31 * 128], BF16, kind="Internal").ap()

    es = contextlib.ExitStack()

    def sb(name, shape, dt):
        return es.enter_context(nc.sbuf_tensor(name, list(shape), dt))

    X = sb("X", [128, NCH, TPB], F32)
    H = sb("H", [128, NCH, TPB], BF16)
    R32 = sb("R32", [128, NCH * TPB], F32)
    PC = sb("PC", [128, NCH, TPB], BF16)
    M = sb("M", [128, NCH, TPB], BF16)
    RING = sb("RING", [128, NRING, NCH, GC], BF16)
    DG = sb("DG", [128, 2, 31, 128], BF16)
    NTF = 19
    NTB = 7
    TF = sb("TF", [128, NTF, 516], F32)
    TB = sb("TB", [128, NTB, 576], BF16)
    VT = sb("VT", [128, NCH, 256], F32)
    CL = sb("CL", [128, NLAYERS, NCH], F32)
    CLT = sb("CLT", [128, NLAYERS, NCH], F32)
    CL2 = sb("CL2", [128, NLAYERS, NCH], F32)
    IDF = sb("IDF", [128, 128], F32)
    IDB = sb("IDB", [128, 128], BF16)
    ONESB = sb("ONESB", [128, 128], BF16)
    RG = sb("RG", [128, NLAYERS, NCH, 2, 128], BF16)
    ORW = sb("ORW", [128, 2, D], F32)
    SOUT = sb("SOUT", [128, NCH, NSOUT], F32)
    US = sb("US", [128, NCH, NSEQ, 34], BF16)
    BXS = sb("BXS", [128, NCH, NSEQ, 7], F32)
    CINS = sb("CINS", [128, NCH, NSEQ, 6], F32)
    LRUS = sb("LRUS", [128, NCH, NSEQ], F32)
    HA = sb("HA", [128, NLAYERS, NCH, 30], BF16)
    HB = sb("HB", [128, NLAYERS, NCH, 3], F32)
    HC = sb("HC", [128, NLAYERS, NCH, 2], F32)
    HL = sb("HL", [128, NLAYERS, NCH], F32)
    PS = es.enter_context(nc.psum_tensor("PS", [128, 8, 512], F32))

    Rb = R32[:].bitcast(BF16).rearrange("p (c t) -> p c t", t=TPB)
    O = R32[:].rearrange("p (c t) -> p c t", t=TPB)

    sems = {}
    for e in Tracker.ENGS:
        sems[("eng", e)] = es.enter_context(nc.semaphore("s_" + e))

    def dsem(name):
        k = ("dma", name)
        if k not in sems:
            sems[k] = es.enter_context(nc.semaphore("d_" + str(name)))
        return name

    def kw_act(bias, scale):
        kw = {}
        if bias is not None:
            kw["bias"] = bias
        if scale is not None:
            kw["scale"] = scale
        return kw

    def ACT(out, in_, func, reads, writes, bias=None, scale=None):
        kw = kw_act(bias, scale)
        return T.op("act", lambda e: e.activation(out=out, in_=in_, func=func, **kw), reads, writes)

    def TT(out, in0, in1, op, reads, writes, eng="dve"):
        return T.op(eng, lambda e: e.tensor_tensor(out, in0, in1, op), reads, writes)

    def STT(out, in0, scalar, in1, op0, op1, reads, writes, eng="dve"):
        return T.op(eng, lambda e: e.scalar_tensor_tensor(out, in0, scalar, in1, op0, op1), reads, writes)

    def TS(out, in0, s1, s2, op0, op1, reads, writes, eng="dve"):
        if s2 is None:
            return T.op(eng, lambda e: e.tensor_scalar(out, in0, s1, None, op0), reads, writes)
        return T.op(eng, lambda e: e.tensor_scalar(out, in0, s1, s2, op0, op1), reads, writes)

    def CP(out, in_, reads, writes, eng="dve"):
        if eng == "act":
            return T.op(eng, lambda e: e.copy(out, in_), reads, writes)
        return T.op(eng, lambda e: e.tensor_copy(out, in_), reads, writes)

    def MM(out, lhsT, rhs, start, stop, reads, writes):
        return T.op("pe", lambda e: e.matmul(out, lhsT, rhs, start=start, stop=stop), reads, writes)

    def TR(out, in_, ident, reads, writes):
        return T.op("pe", lambda e: e.transpose(out, in_, ident), reads, writes)

    def DMA(eng, out, in_, semname, reads, writes, **kw):
        dsem(semname)
        return T.op(eng, lambda e: e.dma_start(out=out, in_=in_, **kw), reads, writes, dma_sem=semname)

    misc_rr = [0]

    def misc_sem():
        misc_rr[0] = (misc_rr[0] + 1) % 8
        return "m%d" % misc_rr[0]

    bank_free = list(range(8))

    def bank():
        if not bank_free:
            raise RuntimeError("PSUM bank liveness exceeded 8")
        return bank_free.pop(0)

    def bfree(*bs):
        for b in bs:
            assert b not in bank_free
            bank_free.append(b)

    def hold_bank():
        return bank()

    def unhold(b):
        pass

    def psk(b):
        return ("ps", b)

    def tf(i, n, off=0):
        return TF[:, i, off:off + n]

    def tfk(i):
        return ("tf", i)

    def tb(i, n):
        return TB[:, i, 0:n]

    def tbk(i):
        return ("tb", i)

    def vt(l, row, c):
        return VT[:, c, l * NVEC + row: l * NVEC + row + 1]

    def b_in(l, colbase, c):
        j = colbase // 128 + c
        return VT[:, j % 8, l * NVEC + V_BIN + j // 8: l * NVEC + V_BIN + j // 8 + 1]

    def wsrc(name, l, col0):
        if name == "w_in":
            v = w_in[l].rearrange("(k p) c -> p k c", p=128)
        else:
            v = {"w_a_out": w_a_out, "w_b_out": w_b_out, "w_c_out": w_c_out, "w_o": w_o}[name][l] \
                .rearrange("(k p) c -> p k c", p=128)
        return v[:, :, col0:col0 + GC]

    ws = {"free": list(range(NRING)), "slot_of": {}}

    def ws_get(p_i, l, name, col0):
        key = (p_i, l, name, col0)
        if key not in ws["slot_of"]:
            slot = ws["free"].pop(0)
            ws["slot_of"][key] = slot
            DMA("pool", RING[:, slot], wsrc(name, l, col0), "wr%d" % slot, [], [("wr", slot)])
        return ws["slot_of"][key]

    def ws_release(p_i, l, name, col0):
        slot = ws["slot_of"].pop((p_i, l, name, col0))
        ws["free"].append(slot)

    DMA("sp", IDF[:], ident_d, misc_sem(), [], [("IDF",)])
    CP(IDB[:], IDF[:], [("IDF",)], [("IDB",)])
    T.op("dve", lambda e: e.memset(ONESB[:], 1.0), [], [("ONESB",)])
    T.op("dve", lambda e: e.memset(HA[:], 0.0), [], [("HA",)])
    T.op("dve", lambda e: e.memset(HB[:], 0.0), [], [("HB",)])
    T.op("dve", lambda e: e.memset(HC[:], 0.0), [], [("HC",)])
    T.op("dve", lambda e: e.memset(HL[:], 0.0), [], [("HL",)])
    T.op("dve", lambda e: e.memset(RG[:], 0.0), [], [("RG",)])

    for rt in range(2):
        DMA("sp", ORW[:, rt, :], vecs[rt * 128:(rt + 1) * 128, :], misc_sem(), [], [("ORW", rt)])
        for half in range(2):
            b = bank()
            for j in range(4):
                c = half * 4 + j
                TR(PS[:, b, j * 128:(j + 1) * 128], ORW[:, rt, c * 128:(c + 1) * 128], IDF[:],
                   [("ORW", rt), ("IDF",)], [psk(b)])
            ACT(VT[:, half * 4:half * 4 + 4, rt * 128:(rt + 1) * 128],
                PS[:, b, :].rearrange("p (j r) -> p j r", r=128), AF.Identity,
                [psk(b)], [("VT",)])
            bfree(b)
    for l in range(NLAYERS):
        lam = VT[:, :, l * NVEC + V_LAM]
        ACT(CLT[:, l, :], lam, AF.Exp, [("VT",)], [("CLT",)], scale=-1.0)
    ACT(CLT[:], CLT[:], AF.Ln, [("CLT",)], [("CLT",)], bias=1.0)
    TS(CL[:], CLT[:], -8.0, None, ALU.mult, None, [("CLT",)], [("CL",)])
    TS(CL2[:], CLT[:], -16.0, None, ALU.mult, None, [("CLT",)], [("CL",)])

    def rg_load(l):
        for gi, rw in enumerate((rg_w_a, rg_w_x)):
            for half in range(2):
                src = rw[l].rearrange("(c h) i j -> h i c j", h=2)[half]
                DMA("pool", RG[half * 64:(half + 1) * 64, l, :, gi, half * 64:(half + 1) * 64], src,
                    "rg%d" % (gi * 2 + half), [("RG",)], [("RG", l, gi, half)])

    def run_pipeline(gens):
        active = []
        for g in gens:
            active.append(g)
            for a in list(active):
                try:
                    next(a)
                except StopIteration:
                    active.remove(a)
        while active:
            for a in list(active):
                try:
                    next(a)
                except StopIteration:
                    active.remove(a)

    def proj(p_i, l, name, col0, cc, rhs_buf, rhs_keyf, off, nt):
        slot = ws_get(p_i, l, name, col0)
        b = bank()
        for k in range(NCH):
            MM(PS[:, b, 0:nt], RING[:, slot, k, cc * 128:(cc + 1) * 128], rhs_buf[:, k, off:off + nt],
               k == 0, k == NCH - 1, [("wr", slot), rhs_keyf(k)], [psk(b)])
        return b

    def proj_multi(p_i, l, name, col0, cc, rhs_buf, rhs_keyf, utiles):
        slot = ws_get(p_i, l, name, col0)
        banks = {t[0]: bank() for t in utiles}
        for k in range(NCH):
            for (ti, st, nt, kind, off) in utiles:
                MM(PS[:, banks[ti], 0:nt], RING[:, slot, k, cc * 128:(cc + 1) * 128], rhs_buf[:, k, off:off + nt],
                   k == 0, k == NCH - 1, [("wr", slot), rhs_keyf(k)], [psk(banks[ti])])
        return banks

    def stats_rs(b, nt, out_i):
        ACT(tf(out_i, nt), PS[:, b, 0:nt], AF.Ln, [psk(b)], [tfk(out_i)], bias=EPS, scale=1.0 / D)
        ACT(tf(out_i, nt), tf(out_i, nt), AF.Exp, [tfk(out_i)], [tfk(out_i)], scale=-0.5)

    for p_i, pss in enumerate(passes):
        tiles = []
        base = TILES[pss[0]][0]
        for ti in pss:
            st, nt, kind = TILES[ti]
            tiles.append((ti, st, nt, kind, st - base))
        is_last_pass = (pss[-1] == len(TILES) - 1)

        def Xk(c):
            return ("X", c)

        def Hk(c):
            return ("H", c)

        def Rk(j):
            return ("R", j)

        def PCk(c):
            return ("PC", c)

        def Mk(c):
            return ("M", c)

        for (ti, st, nt, kind, off) in tiles:
            for r0 in range(0, nt, 128):
                nr = min(128, nt - r0)
                slot = (r0 // 128) % 2
                DMA("sp", ORW[0:nr, slot, :], xin[st + r0: st + r0 + nr, :], misc_sem(), [], [("ORW", slot)])
                for half in range(2):
                    b = bank()
                    for j in range(4):
                        c = half * 4 + j
                        TR(PS[:, b, j * 128: j * 128 + nr], ORW[0:nr, slot, c * 128:(c + 1) * 128],
                           IDF[0:nr, 0:nr], [("ORW", slot), ("IDF",)], [psk(b)])
                    ACT(X[:, half * 4:half * 4 + 4, off + r0: off + r0 + nr],
                        PS[:, b, :].rearrange("p (j r) -> p j r", r=128)[:, :, 0:nr], AF.Identity,
                        [psk(b)], [Xk(half * 4 + j) for j in range(4)])
                    bfree(b)

        if p_i == 0:
            for l in range(nl):
                DMA("sp", sa_old[l], sa_in[l, 4:30], misc_sem(), [], [])

        for l in range(nl):
            VB = l * NVEC
            if p_i == 0:
                rg_load(l)
            if is_last_pass:
                for i in range(4):
                    nr = 128 if i < 3 else 96
                    nj = nr // 16
                    slot = i % 2
                    DMA("sp", ORW[0:nr, slot, :],
                        sa_in[l].rearrange("j s d -> (j s) d")[i * 128: i * 128 + nr, :],
                        misc_sem(), [], [("ORW", slot)])
                    for half in range(2):
                        b = bank()
                        for j in range(4):
                            c = half * 4 + j
                            TR(PS[:, b, j * 128: j * 128 + nr], ORW[0:nr, slot, c * 128:(c + 1) * 128],
                               IDF[0:nr, 0:nr], [("ORW", slot), ("IDF",)], [psk(b)])
                        for j in range(4):
                            c = half * 4 + j
                            ACT(US[:, c, :, i * 8: i * 8 + nj],
                                PS[:, b, j * 128: j * 128 + nr].rearrange("p (j s) -> p s j", s=NSEQ),
                                AF.Identity, [psk(b)], [("US", c)])
                        bfree(b)
                slot = 0
                DMA("sp", ORW[0:96, slot, :], sbc_in[l], misc_sem(), [], [("ORW", slot)])
                for half in range(2):
                    b = bank()
                    for j in range(4):
                        c = half * 4 + j
                        TR(PS[:, b, j * 128: j * 128 + 96], ORW[0:96, slot, c * 128:(c + 1) * 128],
                           IDF[0:96, 0:96], [("ORW", slot), ("IDF",)], [psk(b)])
                    for j in range(4):
                        c = half * 4 + j
                        ACT(BXS[:, c, :, 0:3], PS[:, b, j * 128: j * 128 + 48].rearrange("p (j s) -> p s j", s=NSEQ),
                            AF.Identity, [psk(b)], [("BXS", c)])
                        ACT(LRUS[:, c, :], PS[:, b, j * 128 + 48: j * 128 + 64],
                            AF.Identity, [psk(b)], [("LRUS", c)])
                        ACT(CINS[:, c, :, 0:2], PS[:, b, j * 128 + 64: j * 128 + 96].rearrange("p (j s) -> p s j", s=NSEQ),
                            AF.Identity, [psk(b)], [("CINS", c)])
                    bfree(b)

            MARK(('PRE', p_i, l if 'PRE' != 'Y' else -1))
            for (ti, st, nt, kind, off) in tiles:
                bs = hold_bank()
                for c in range(NCH):
                    i = c % 2
                    ACT(tb(i, nt), X[:, c, off:off + nt], AF.Square, [Xk(c)], [tbk(i)])
                    MM(PS[:, bs, 0:nt], ONESB[:], tb(i, nt), c == 0, c == NCH - 1, [tbk(i), ("ONESB",)], [psk(bs)])
                stats_rs(bs, nt, 0)
                bfree(bs)
                for c in range(NCH):
                    STT(H[:, c, off:off + nt], X[:, c, off:off + nt], vt(l, V_NPRE, c), tf(0, nt),
                        ALU.mult, ALU.mult, [Xk(c), tfk(0), ("VT",)], [Hk(c)])

            if p_i == 0 and l == 0:
                DBG("H", H[:], [128, NCH, TPB], BF16, [Hk(c) for c in range(NCH)])
            MARK(('A1', p_i, l if 'A1' != 'Y' else -1))
            def a1_unit(cs, utiles, par):
                q = cs[0] // 2
                items = [(c, t) for c in cs for t in utiles]
                rel = (cs[-1] % 2 == 1) and (utiles[-1] is tiles[-1])
                bv = {}
                bg = {}
                for c in cs:
                    bv[c] = proj_multi(p_i, l, "w_in", C_AV + q * GC, c % 2, H, Hk, utiles)
                    bg[c] = proj_multi(p_i, l, "w_in", C_AG + q * GC, c % 2, H, Hk, utiles)
                if rel:
                    ws_release(p_i, l, "w_in", C_AV + q * GC)
                    ws_release(p_i, l, "w_in", C_AG + q * GC)
                yield
                for j, (c, (ti, st, nt, kind, off)) in enumerate(items):
                    si = par * 2 + j
                    ACT(tf(si, nt), PS[:, bg[c][ti], 0:nt], AF.Sigmoid, [psk(bg[c][ti])], [tfk(si)],
                        bias=b_in(l, C_AG, c))
                for j, (c, (ti, st, nt, kind, off)) in enumerate(items):
                    si = par * 2 + j
                    bvb = bv[c][ti]
                    last_prompt = (kind == 'p' and ti == 3)
                    if kind == 'p':
                        ui = par * 2 + j
                        CP(TB[:, ui, 0:30], HA[:, l, c, :], [("HA", l, c)], [tbk(ui)], eng="act")
                        STT(TB[:, ui, 30:30 + nt], PS[:, bvb, 0:nt], b_in(l, C_AV, c), tf(si, nt),
                            ALU.add, ALU.mult, [psk(bvb), tfk(si), ("VT",)], [tbk(ui)])
                        if last_prompt:
                            STT(SOUT[:, c, 0:30], PS[:, bvb, nt - 30:nt], b_in(l, C_AV, c), tf(si, 30, nt - 30),
                                ALU.add, ALU.mult, [psk(bvb), tfk(si), ("VT",)], [("SOUT", c)])
                    else:
                        STT(US[:, c, :, 30:34], PS[:, bvb, 0:nt].rearrange("p (s t) -> p s t", t=4),
                            b_in(l, C_AV, c), tf(si, nt).rearrange("p (s t) -> p s t", t=4),
                            ALU.add, ALU.mult, [psk(bvb), tfk(si), ("VT",)], [("US", c)])
                        STT(SOUT[:, c, 36:100].rearrange("p (s t) -> p s t", t=4),
                            PS[:, bvb, 0:nt].rearrange("p (s t) -> p s t", t=4),
                            b_in(l, C_AV, c), tf(si, nt).rearrange("p (s t) -> p s t", t=4),
                            ALU.add, ALU.mult, [psk(bvb), tfk(si), ("VT",)], [("SOUT", c)])
                    bfree(bvb, bg[c][ti])
                yield
                for c in cs:
                    di = c % 2
                    if utiles[0] is tiles[0]:
                        if p_i == 0 and l == 0:
                            TT(DG[:, di], IDB[:].unsqueeze(1).to_broadcast([128, 31, 128]),
                               VT[:, c, VB + V_CAW: VB + V_CAW + 31].unsqueeze(2).to_broadcast([128, 31, 128]),
                               ALU.mult, [("IDB",), ("VT",)], [("DG", di)], eng=DG_ENG)
                            if len(passes) > 1:
                                DMA("sp", dgs[l, c], DG[:, di].rearrange("p k m -> p (k m)"), "dgst%d" % di,
                                    [("DG", di)], [("dgs", l, c)])
                        else:
                            DMA("sp", DG[:, di].rearrange("p k m -> p (k m)"), dgs[l, c], "dgld%d" % di,
                                [("dgs", l, c)], [("DG", di)])
                for j, (c, (ti, st, nt, kind, off)) in enumerate(items):
                    if kind == 'p' and ti != 3:
                        ui = par * 2 + j
                        CP(HA[:, l, c, :], TB[:, ui, nt:nt + 30], [tbk(ui)], [("HA", l, c)], eng="act")
                yield
                bc = {}
                for c in cs:
                    di = c % 2
                    its = [(j, it) for j, it in enumerate(items) if it[0] == c]
                    for j, (c_, (ti, st, nt, kind, off)) in its:
                        bc[(c, ti)] = bank()
                    for k in range(31):
                        for j, (c_, (ti, st, nt, kind, off)) in its:
                            if kind == 'p':
                                rhs = TB[:, par * 2 + j, k:k + nt]
                                rk = tbk(par * 2 + j)
                            else:
                                rhs = US[:, c, :, k:k + 4]
                                rk = ("US", c)
                            MM(PS[:, bc[(c, ti)], 0:nt], DG[:, di, k, :], rhs, k == 0, k == 30,
                               [("DG", di), rk], [psk(bc[(c, ti)])])
                yield
                for j, (c, (ti, st, nt, kind, off)) in enumerate(items):
                    ACT(Rb[:, c, off:off + nt], PS[:, bc[(c, ti)], 0:nt], AF.Identity, [psk(bc[(c, ti)])], [Rk(c)],
                        bias=vt(l, V_CAB, c))
                    bfree(bc[(c, ti)])

            def b_unit(c, tile, par):
                ti, st, nt, kind, off = tile
                last_prompt = (kind == 'p' and ti == 3)
                q, cc = c // 2, c % 2
                bi = 4 + par * 5
                xi = bi + 1
                ri = bi + 2
                ai = bi + 3
                zi = bi + 4
                xbb = 4 + par
                bx = proj(p_i, l, "w_in", C_BX + q * GC, cc, H, Hk, off, nt)
                if tile is tiles[-1] and cc == 1:
                    ws_release(p_i, l, "w_in", C_BX + q * GC)
                yield
                if kind == 'p':
                    CP(TF[:, bi, 0:3], HB[:, l, c, :], [("HB", l, c)], [tfk(bi)], eng="act")
                    ACT(TF[:, bi, 3:3 + nt], PS[:, bx, 0:nt], AF.Identity, [psk(bx), ("VT",)], [tfk(bi)],
                        bias=b_in(l, C_BX, c))
                    taps = [TF[:, bi, k:k + nt] for k in range(4)]
                    xb = tf(xi, nt)
                    rk = tfk(bi)
                else:
                    ACT(BXS[:, c, :, 3:7], PS[:, bx, 0:nt].rearrange("p (s t) -> p s t", t=4), AF.Identity,
                        [psk(bx), ("VT",)], [("BXS", c)], bias=b_in(l, C_BX, c))
                    taps = [BXS[:, c, :, k:k + 4] for k in range(4)]
                    xb = tf(xi, nt).rearrange("p (s t) -> p s t", t=4)
                    rk = ("BXS", c)
                bfree(bx)
                TS(xb, taps[3], vt(l, V_CBW + 3, c), vt(l, V_CBB, c), ALU.mult, ALU.add,
                   [rk, ("VT",)], [tfk(xi)])
                for k in (2, 1, 0):
                    STT(xb, taps[k], vt(l, V_CBW + k, c), xb, ALU.mult, ALU.add,
                        [rk, tfk(xi), ("VT",)], [tfk(xi)])
                yield
                CP(tb(xbb, nt), tf(xi, nt), [tfk(xi)], [tbk(xbb)], eng="act")
                if kind == 'p':
                    if last_prompt:
                        CP(SOUT[:, c, 30:33], TF[:, bi, nt:nt + 3], [tfk(bi)], [("SOUT", c)], eng="act")
                    else:
                        CP(HB[:, l, c, :], TF[:, bi, nt:nt + 3], [tfk(bi)], [("HB", l, c)], eng="act")
                else:
                    CP(SOUT[:, c, 100:148].rearrange("p (s j) -> p s j", j=3), BXS[:, c, :, 4:7],
                       [("BXS", c)], [("SOUT", c)], eng="act")
                yield
                br = bank()
                MM(PS[:, br, 0:nt], RG[:, l, c, 0, :], tb(xbb, nt), True, True,
                   [("RG", l, 0, 0), ("RG", l, 0, 1), tbk(xbb)], [psk(br)])
                bi_ = bank()
                MM(PS[:, bi_, 0:nt], RG[:, l, c, 1, :], tb(xbb, nt), True, True,
                   [("RG", l, 1, 0), ("RG", l, 1, 1), tbk(xbb)], [psk(bi_)])
                bz = proj(p_i, l, "w_in", C_BZ + q * GC, cc, H, Hk, off, nt)
                if tile is tiles[-1] and cc == 1:
                    ws_release(p_i, l, "w_in", C_BZ + q * GC)
                yield
                ACT(tf(ri, nt), PS[:, br, 0:nt], AF.Sigmoid, [psk(br), ("VT",)], [tfk(ri)], bias=vt(l, V_RBA, c))
                ACT(tf(bi, nt), PS[:, bi_, 0:nt], AF.Sigmoid, [psk(bi_), ("VT",)], [tfk(bi)], bias=vt(l, V_RBX, c))
                ACT(tf(zi, nt), PS[:, bz, 0:nt], AF.Sigmoid, [psk(bz), ("VT",)], [tfk(zi)], bias=b_in(l, C_BZ, c))
                ACT(tf(ai, nt), tf(ri, nt), AF.Exp, [tfk(ri), ("CL",)], [tfk(ai)], scale=CL[:, l, c:c + 1])
                ACT(tf(ri, nt), tf(ri, nt), AF.Exp, [tfk(ri), ("CL",)], [tfk(ri)], scale=CL2[:, l, c:c + 1])
                ACT(tf(ri, nt), tf(ri, nt), AF.Ln, [tfk(ri)], [tfk(ri)], bias=1.0, scale=-1.0)
                ACT(tf(ri, nt), tf(ri, nt), AF.Exp, [tfk(ri)], [tfk(ri)], scale=0.5)
                bfree(br, bi_)
                TT(tf(bi, nt), tf(bi, nt), tf(xi, nt), ALU.mult, [tfk(bi), tfk(xi)], [tfk(bi)])
                TT(tf(ri, nt), tf(ri, nt), tf(bi, nt), ALU.mult, [tfk(ri), tfk(bi)], [tfk(ri)])
                if kind == 'p':
                    T.op("dve", lambda e, o=tf(xi, nt), a=tf(ai, nt), u=tf(ri, nt), h0=HL[:, l, c:c + 1]:
                         e.tensor_tensor_scan(o, a, u, h0, ALU.mult, ALU.add),
                         [tfk(ai), tfk(ri), ("HL", l, c)], [tfk(xi)])
                    if last_prompt:
                        CP(SOUT[:, c, 33:34], TF[:, xi, nt - 1:nt], [tfk(xi)], [("SOUT", c)], eng="dve")
                    else:
                        CP(HL[:, l, c:c + 1], TF[:, xi, nt - 1:nt], [tfk(xi)], [("HL", l, c)], eng="dve")
                    hs_keys = [tfk(xi)]
                else:
                    for s_ in range(NSEQ):
                        T.op("dve", lambda e, o=TF[:, xi, s_ * 4:s_ * 4 + 4], a=TF[:, ai, s_ * 4:s_ * 4 + 4],
                             u=TF[:, ri, s_ * 4:s_ * 4 + 4], h0=LRUS[:, c, s_:s_ + 1]:
                             e.tensor_tensor_scan(o, a, u, h0, ALU.mult, ALU.add),
                             [tfk(ai), tfk(ri), ("LRUS", c)], [("sub", xi, s_)])
                    hs_keys = [tfk(xi)] + [("sub", xi, s_) for s_ in range(NSEQ)]
                    CP(SOUT[:, c, 148:164], tf(xi, nt).rearrange("p (s t) -> p s t", t=4)[:, :, 3],
                       hs_keys, [("SOUT", c)], eng="dve")
                STT(tf(xi, nt), PS[:, bz, 0:nt], b_in(l, C_BZ, c), tf(xi, nt), ALU.add, ALU.mult,
                    hs_keys + [psk(bz), ("VT",)], [tfk(xi)])
                bfree(bz)
                TT(Rb[:, 8 + c, off:off + nt], tf(xi, nt), tf(zi, nt), ALU.mult, [tfk(xi), tfk(zi)], [Rk(8 + c)])

            def unit_groups(q):
                if len(tiles) == 1:
                    return [((2 * q, 2 * q + 1), tiles)]
                return [((2 * q,), tiles), ((2 * q + 1,), tiles)]

            units = []
            nA = 0
            nB = 0
            for q in range(4):
                for (cs_, ut_) in [((2 * q,), tiles), ((2 * q + 1,), tiles)]:
                    units.append(a1_unit(cs_, ut_, nA % 2))
                    nA += 1
                for c in (2 * q, 2 * q + 1):
                    for tile in tiles:
                        units.append(b_unit(c, tile, nB % 3))
                        nB += 1
            run_pipeline(units)

            if p_i == 0 and l == 0:
                DBG("V", R32[:], [128, NCH * TPB], F32, [Rk(c) for c in range(16)])
            MARK(('A2', p_i, l if 'A2' != 'Y' else -1))
            ln_slots = {}
            for tix, (ti, st, nt, kind, off) in enumerate(tiles):
                b1 = hold_bank()
                b2 = hold_bank()
                for c in range(NCH):
                    i = c % 2
                    ACT(tb(i, nt), Rb[:, c, off:off + nt], AF.Square, [Rk(c)], [tbk(i)])
                    MM(PS[:, b1, 0:nt], ONESB[:], Rb[:, c, off:off + nt], c == 0, c == NCH - 1,
                       [Rk(c), ("ONESB",)], [psk(b1)])
                    MM(PS[:, b2, 0:nt], ONESB[:], tb(i, nt), c == 0, c == NCH - 1,
                       [tbk(i), ("ONESB",)], [psk(b2)])
                mu_i = 10 + 2 * tix
                rs_i = 11 + 2 * tix
                ln_slots[ti] = (mu_i, rs_i)
                T.op("act", lambda e, o=tf(mu_i, nt), i_=PS[:, b1, 0:nt]: e.mul(o, i_, 1.0 / D),
                     [psk(b1)], [tfk(mu_i)])
                TT(tf(rs_i, nt), tf(mu_i, nt), tf(mu_i, nt), ALU.mult, [tfk(mu_i)], [tfk(rs_i)])
                STT(tf(rs_i, nt), PS[:, b2, 0:nt], 1.0 / D, tf(rs_i, nt), ALU.mult, ALU.subtract,
                    [psk(b2), tfk(rs_i)], [tfk(rs_i)])
                bfree(b1, b2)
                ACT(tf(rs_i, nt), tf(rs_i, nt), AF.Ln, [tfk(rs_i)], [tfk(rs_i)], bias=EPS)
                ACT(tf(rs_i, nt), tf(rs_i, nt), AF.Exp, [tfk(rs_i)], [tfk(rs_i)], scale=-0.5)

            MARK(('A3', p_i, l if 'A3' != 'Y' else -1))
            def a3_unit(cs, utiles, par):
                q = cs[0] // 2
                items = [(c, t) for c in cs for t in utiles]
                rel = (cs[-1] % 2 == 1) and (utiles[-1] is tiles[-1])
                bz = {}
                for c in cs:
                    bz[c] = proj_multi(p_i, l, "w_in", C_AZ + q * GC, c % 2, H, Hk, utiles)
                if rel:
                    ws_release(p_i, l, "w_in", C_AZ + q * GC)
                yield
                for j, (c, (ti, st, nt, kind, off)) in enumerate(items):
                    mu_i, rs_i = ln_slots[ti]
                    ni = par * 2 + j
                    TT(tf(ni, nt), Rb[:, c, off:off + nt], tf(mu_i, nt), ALU.subtract, [Rk(c), tfk(mu_i)], [tfk(ni)])
                    TT(tf(ni, nt), tf(ni, nt), tf(rs_i, nt), ALU.mult, [tfk(ni), tfk(rs_i)], [tfk(ni)])
                for j, (c, (ti, st, nt, kind, off)) in enumerate(items):
                    ni = par * 2 + j
                    zi = 4 + par * 2 + j
                    ACT(tf(ni, nt), tf(ni, nt), AF.Silu, [tfk(ni), ("VT",)], [tfk(ni)],
                        bias=vt(l, V_LNB, c), scale=vt(l, V_LNG, c))
                    ACT(tf(zi, nt), PS[:, bz[c][ti], 0:nt], AF.Silu, [psk(bz[c][ti]), ("VT",)], [tfk(zi)],
                        bias=b_in(l, C_AZ, c))
                    bfree(bz[c][ti])
                for j, (c, (ti, st, nt, kind, off)) in enumerate(items):
                    ni = par * 2 + j
                    zi = 4 + par * 2 + j
                    TT(Rb[:, c, off:off + nt], tf(ni, nt), tf(zi, nt), ALU.mult, [tfk(ni), tfk(zi)], [Rk(c)])

            units = []
            n = 0
            for q in range(4):
                for (cs_, ut_) in unit_groups(q):
                    units.append(a3_unit(cs_, ut_, n % 2))
                    n += 1
            run_pipeline(units)

            if p_i == 0 and l == 0:
                DBG("MU", TF[:, 10:12, :], [128, 2, 516], F32, [tfk(10), tfk(11)])
                DBG("PA", R32[:], [128, NCH * TPB], F32, [Rk(c) for c in range(16)])
            MARK(('C', p_i, l if 'C' != 'Y' else -1))
            def c_unit(cs, utiles, par):
                q = cs[0] // 2
                items = [(c, t) for c in cs for t in utiles]
                rel = (cs[-1] % 2 == 1) and (utiles[-1] is tiles[-1])
                bcc = {}
                bcx = {}
                for c in cs:
                    bcc[c] = proj_multi(p_i, l, "w_in", C_CC + q * GC, c % 2, H, Hk, utiles)
                    bcx[c] = proj_multi(p_i, l, "w_in", C_CX + q * GC, c % 2, H, Hk, utiles)
                if rel:
                    ws_release(p_i, l, "w_in", C_CC + q * GC)
                    ws_release(p_i, l, "w_in", C_CX + q * GC)
                yield
                for j, (c, (ti, st, nt, kind, off)) in enumerate(items):
                    last_prompt = (kind == 'p' and ti == 3)
                    ci = par * 2 + j
                    ni = 4 + par * 2 + j
                    xi = 8 + par * 2 + j
                    b1, b2 = bcc[c][ti], bcx[c][ti]
                    ACT(tf(ci, nt), PS[:, b1, 0:nt], AF.Identity, [psk(b1), ("VT",)], [tfk(ci)],
                        bias=b_in(l, C_CC, c))
                    if kind == 'p':
                        CP(TF[:, ni, 0:2], HC[:, l, c, :], [("HC", l, c)], [tfk(ni)], eng="act")
                        STT(TF[:, ni, 2:2 + nt], PS[:, b2, 0:nt], b_in(l, C_CX, c), tf(ci, nt),
                            ALU.add, ALU.mult, [psk(b2), tfk(ci), ("VT",)], [tfk(ni)])
                        if last_prompt:
                            CP(SOUT[:, c, 34:36], TF[:, ni, nt:nt + 2], [tfk(ni)], [("SOUT", c)], eng="dve")
                        else:
                            CP(HC[:, l, c, :], TF[:, ni, nt:nt + 2], [tfk(ni)], [("HC", l, c)], eng="dve")
                        taps = [TF[:, ni, k:k + nt] for k in range(3)]
                        cx = tf(xi, nt)
                        rk = tfk(ni)
                    else:
                        STT(CINS[:, c, :, 2:6], PS[:, b2, 0:nt].rearrange("p (s t) -> p s t", t=4),
                            b_in(l, C_CX, c), tf(ci, nt).rearrange("p (s t) -> p s t", t=4),
                            ALU.add, ALU.mult, [psk(b2), tfk(ci), ("VT",)], [("CINS", c)])
                        CP(SOUT[:, c, 164:196].rearrange("p (s j) -> p s j", j=2), CINS[:, c, :, 4:6],
                           [("CINS", c)], [("SOUT", c)], eng="dve")
                        taps = [CINS[:, c, :, k:k + 4] for k in range(3)]
                        cx = tf(xi, nt).rearrange("p (s t) -> p s t", t=4)
                        rk = ("CINS", c)
                    TS(cx, taps[2], vt(l, V_CCW + 2, c), None, ALU.mult, None, [rk, ("VT",)], [tfk(xi)])
                    for k in (1, 0):
                        STT(cx, taps[k], vt(l, V_CCW + k, c), cx, ALU.mult, ALU.add,
                            [rk, tfk(xi), ("VT",)], [tfk(xi)])
                    bfree(b1, b2)
                yield
                bcb = {}
                bcz = {}
                for c in cs:
                    bcb[c] = proj_multi(p_i, l, "w_in", C_CB + q * GC, c % 2, H, Hk, utiles)
                    bcz[c] = proj_multi(p_i, l, "w_in", C_CZ + q * GC, c % 2, H, Hk, utiles)
                if rel:
                    ws_release(p_i, l, "w_in", C_CB + q * GC)
                    ws_release(p_i, l, "w_in", C_CZ + q * GC)
                yield
                for j, (c, (ti, st, nt, kind, off)) in enumerate(items):
                    ci = par * 2 + j
                    xi = 8 + par * 2 + j
                    b1, b2 = bcb[c][ti], bcz[c][ti]
                    STT(tf(xi, nt), PS[:, b1, 0:nt], b_in(l, C_CB, c), tf(xi, nt), ALU.add, ALU.mult,
                        [psk(b1), tfk(xi), ("VT",)], [tfk(xi)])
                    ACT(tf(ci, nt), PS[:, b2, 0:nt], AF.Silu, [psk(b2), ("VT",), tfk(ci)], [tfk(ci)],
                        bias=b_in(l, C_CZ, c))
                    TT(PC[:, c, off:off + nt], tf(xi, nt), tf(ci, nt), ALU.mult, [tfk(xi), tfk(ci)], [PCk(c)])
                    bfree(b1, b2)

            units = []
            n = 0
            for q in range(4):
                for (cs_, ut_) in unit_groups(q):
                    units.append(c_unit(cs_, ut_, n % 2))
                    n += 1
            run_pipeline(units)

            MARK(('SO', p_i, l if 'SO' != 'Y' else -1))
            if is_last_pass:
                for c in range(NCH):
                    for (c0, ncol, slot) in ((0, 128, 0), (128, NSOUT - 128, 1)):
                        pass
                for (c0, ncol, slot) in ((0, 128, 0), (128, NSOUT - 128, 1)):
                    for half in range(2):
                        b = bank()
                        for j in range(4):
                            c = half * 4 + j
                            TR(PS[0:ncol, b, j * 128:(j + 1) * 128], SOUT[:, c, c0:c0 + ncol], IDF[:],
                               [("SOUT", c), ("IDF",)], [psk(b)])
                        ACT(ORW[0:ncol, slot, half * 512:(half + 1) * 512], PS[0:ncol, b, :], AF.Identity,
                            [psk(b)], [("ORW", slot)])
                        bfree(b)
                    DMA("sp", st_out[l, c0:c0 + ncol, :], ORW[0:ncol, slot, :], misc_sem(), [("ORW", slot)], [])

            if p_i == 0 and l == 0:
                DBG("PB", R32[:], [128, NCH * TPB], F32, [Rk(c) for c in range(16)])
                DBG("PC", PC[:], [128, NCH, TPB], BF16, [PCk(c) for c in range(NCH)])
            MARK(('MERGE', p_i, l if 'MERGE' != 'Y' else -1))
            def m_unit(cs, utiles, par, jb):
                q = cs[0] // 2
                items = [(c, t) for c in cs for t in utiles]
                rel = (cs[-1] % 2 == 1) and (utiles[-1] is tiles[-1])
                srcs = (("w_a_out", lambda k: Rk(k), 0), ("w_b_out", lambda k: Rk(8 + k), 8), ("w_c_out", PCk, None))
                name, keyf, rbase = srcs[jb]
                by = {}
                bgt = {}
                slot = ws_get(p_i, l, name, q * GC)
                for c in cs:
                    by[c] = {t[0]: bank() for t in utiles}
                    for k in range(NCH):
                        for (ti, st, nt, kind, off) in utiles:
                            rhs = Rb[:, rbase + k, off:off + nt] if rbase is not None else PC[:, k, off:off + nt]
                            MM(PS[:, by[c][ti], 0:nt], RING[:, slot, k, (c % 2) * 128:(c % 2 + 1) * 128], rhs,
                               k == 0, k == NCH - 1, [("wr", slot), keyf(k)], [psk(by[c][ti])])
                    bgt[c] = proj_multi(p_i, l, "w_in", C_G + jb * 1024 + q * GC, c % 2, H, Hk, utiles)
                if rel:
                    ws_release(p_i, l, name, q * GC)
                    ws_release(p_i, l, "w_in", C_G + jb * 1024 + q * GC)
                yield
                for j, (c, (ti, st, nt, kind, off)) in enumerate(items):
                    gi = (jb % 2) * 2 + j
                    t2 = 8 + (jb % 2) * 2 + j
                    mt = 4 + par * 2 + j
                    b1, b2 = by[c][ti], bgt[c][ti]
                    ACT(tf(gi, nt), PS[:, b2, 0:nt], AF.Sigmoid, [psk(b2), ("VT",)], [tfk(gi)],
                        bias=b_in(l, C_G + jb * 1024, c))
                    if jb == 0:
                        TT(tf(mt, nt), tf(gi, nt), PS[:, b1, 0:nt], ALU.mult, [tfk(gi), psk(b1)], [tfk(mt)])
                    else:
                        TT(tf(t2, nt), tf(gi, nt), PS[:, b1, 0:nt], ALU.mult, [tfk(gi), psk(b1)], [tfk(t2)])
                        if jb == 1:
                            TT(tf(mt, nt), tf(mt, nt), tf(t2, nt), ALU.add, [tfk(mt), tfk(t2)], [tfk(mt)])
                        else:
                            TT(M[:, c, off:off + nt], tf(mt, nt), tf(t2, nt), ALU.add, [tfk(mt), tfk(t2)], [Mk(c)])
                    bfree(b1, b2)

            def diag_unit(ln, c):
                di = c % 2
                TT(DG[:, di], IDB[:].unsqueeze(1).to_broadcast([128, 31, 128]),
                   VT[:, c, ln * NVEC + V_CAW: ln * NVEC + V_CAW + 31].unsqueeze(2).to_broadcast([128, 31, 128]),
                   ALU.mult, [("IDB",), ("VT",)], [("DG", di)], eng=DG_ENG)
                DMA("sp", dgs[ln, c], DG[:, di].rearrange("p k m -> p (k m)"), "dgst%d" % di,
                    [("DG", di)], [("dgs", ln, c)])
                return
                yield

            units = []
            n = 0
            ndg = 0
            for q in range(4):
                for (cs_, ut_) in unit_groups(q):
                    for jb in range(3):
                        units.append(m_unit(cs_, ut_, n % 2, jb))
                        if p_i == 0 and l + 1 < nl and ndg < NCH:
                            units.append(diag_unit(l + 1, ndg))
                            ndg += 1
                    n += 1
            run_pipeline(units)

            if p_i == 0 and l == 0:
                DBG("M", M[:], [128, NCH, TPB], BF16, [Mk(c) for c in range(NCH)])
            MARK(('WO', p_i, l if 'WO' != 'Y' else -1))
            ss_bank = {}
            for (ti, st, nt, kind, off) in tiles:
                ss_bank[ti] = hold_bank()

            def wo_unit(cs, utiles, par):
                q = cs[0] // 2
                items = [(c, t) for c in cs for t in utiles]
                rel = (cs[-1] % 2 == 1) and (utiles[-1] is tiles[-1])
                bo = {}
                for c in cs:
                    bo[c] = proj_multi(p_i, l, "w_o", q * GC, c % 2, M, Mk, utiles)
                if rel:
                    ws_release(p_i, l, "w_o", q * GC)
                yield
                for j, (c, (ti, st, nt, kind, off)) in enumerate(items):
                    si = par * 2 + j
                    b1 = bo[c][ti]
                    ACT(O[:, c, off:off + nt], PS[:, b1, 0:nt], AF.Identity, [psk(b1)],
                        [Rk(2 * c), Rk(2 * c + 1)])
                    ACT(tb(si, nt), PS[:, b1, 0:nt], AF.Square, [psk(b1)], [tbk(si)])
                    MM(PS[:, ss_bank[ti], 0:nt], ONESB[:], tb(si, nt), c == 0, c == NCH - 1,
                       [tbk(si), ("ONESB",)], [psk(ss_bank[ti])])
                    bfree(b1)

            units = []
            n = 0
            for q in range(4):
                for (cs_, ut_) in unit_groups(q):
                    units.append(wo_unit(cs_, ut_, n % 2))
                    n += 1
            run_pipeline(units)
            for tix, (ti, st, nt, kind, off) in enumerate(tiles):
                rs_i = 12 + tix
                stats_rs(ss_bank[ti], nt, rs_i)
                bfree(ss_bank[ti])
                for c in range(NCH):
                    t_i = c % 4
                    TT(tf(t_i, nt), O[:, c, off:off + nt], tf(rs_i, nt), ALU.mult,
                       [Rk(2 * c), Rk(2 * c + 1), tfk(rs_i)], [tfk(t_i)])
                    STT(X[:, c, off:off + nt], tf(t_i, nt), vt(l, V_NPOST, c), X[:, c, off:off + nt],
                        ALU.mult, ALU.add, [tfk(t_i), Xk(c), ("VT",)], [Xk(c)])

        if p_i == 0:
            DBG("O", R32[:], [128, NCH * TPB], F32, [Rk(c) for c in range(16)])
            DBG("X", X[:], [128, NCH, TPB], F32, [Xk(c) for c in range(NCH)])
        MARK(('Y', p_i, l if 'Y' != 'Y' else -1))
        for (ti, st, nt, kind, off) in tiles:
            for r0 in range(0, nt, 128):
                nr = min(128, nt - r0)
                slot = (r0 // 128) % 2
                for half in range(2):
                    b = bank()
                    for j in range(4):
                        c = half * 4 + j
                        TR(PS[0:nr, b, j * 128:(j + 1) * 128], X[:, c, off + r0: off + r0 + nr], IDF[:],
                           [Xk(c), ("IDF",)], [psk(b)])
                    ACT(ORW[0:nr, slot, half * 512:(half + 1) * 512], PS[0:nr, b, :], AF.Identity,
                        [psk(b)], [("ORW", slot)])
                    bfree(b)
                DMA("sp", yout[st + r0: st + r0 + nr, :], ORW[0:nr, slot, :], misc_sem(), [("ORW", slot)], [])

    for e in Tracker.ENGS:
        v = 0
        for I in T.lists[e]:
            if I.inc and not I.is_dma:
                v += 1
            I.val = v

    def emit(engname, eh):
        waited = {}
        for I in T.lists[engname]:
            req = {}
            for d in I.deps:
                if d.is_dma:
                    key = ("dma", d.sem)
                    val = d.semval
                else:
                    if d.eng == "pe" and engname == "pe" and not I.is_dma:
                        continue
                    key = ("eng", d.eng)
                    val = d.val
                if val > req.get(key, 0):
                    req[key] = val
            for key, val in req.items():
                if waited.get(key, 0) >= val:
                    continue
                eh.wait_ge(sems[key], val)
                waited[key] = val
            ins = I.fn(eh)
            if I.is_dma:
                ins.then_inc(sems[("dma", I.sem)], 16)
            elif I.inc:
                ins.then_inc(sems[("eng", engname)], 1)
        if engname == "sp":
            for name, lst in T.dma_sems.items():
                eh.wait_ge(sems[("dma", name)], 16 * len(lst))

    with es:
        with nc.Block() as block:
            @block.tensor
            def _(e):
                emit("pe", e)

            @block.scalar
            def _(e):
                emit("act", e)

            @block.vector
            def _(e):
                emit("dve", e)

            @block.gpsimd
            def _(e):
                emit("pool", e)

            @block.sync
            def _(e):
                emit("sp", e)
    stats = {e: len(T.lists[e]) for e in Tracker.ENGS}
    stats['marks'] = marks
    return nc, stats


def pack_inputs(inp):
    f = lambda a: np.ascontiguousarray(np.asarray(a, dtype=np.float32))
    vec_rows = []
    for l in range(NLAYERS):
        rows = [inp["norm_pre"][l][None], inp["norm_post"][l][None], np.asarray(inp["b_in"][l]).reshape(12, D),
                inp["conv_a_w"][l], inp["conv_a_b"][l][None], inp["ln_a_g"][l][None], inp["ln_a_b"][l][None],
                inp["conv_b_w"][l], inp["conv_b_b"][l][None], inp["rg_b_a"][l][None], inp["rg_b_x"][l][None],
                inp["rg_lambda"][l][None], inp["conv_c_w"][l]]
        vec_rows.append(np.concatenate([np.asarray(r, dtype=np.float32) for r in rows], axis=0))
    vecs = np.zeros((256, D), np.float32)
    vv = np.concatenate(vec_rows, axis=0)
    vecs[:vv.shape[0]] = vv
    shared = {
        "vecs": vecs, "ident": np.eye(128, dtype=np.float32),
        "w_in": f(inp["w_in"]), "w_a_out": f(inp["w_a_out"]), "w_b_out": f(inp["w_b_out"]),
        "w_c_out": f(inp["w_c_out"]), "w_o": f(inp["w_o"]),
        "rg_w_a": f(inp["rg_w_a"]), "rg_w_x": f(inp["rg_w_x"]),
    }
    xp = np.asarray(inp["x_prompt"], np.float32)
    xs = np.asarray(inp["x_sample"], np.float32)
    sa = np.asarray(inp["state_conv_a"], np.float32)
    sbb = np.asarray(inp["state_conv_b"], np.float32)
    sl = np.asarray(inp["state_lru"], np.float32)
    sc = np.asarray(inp["state_conv_c"], np.float32)
    maps = []
    for b in range(NCORES):
        s0, s1 = b * NSEQ, (b + 1) * NSEQ
        xin = np.concatenate([xp[b], xs[s0:s1].reshape(NSEQ * 4, D)], axis=0)
        sa_in = np.ascontiguousarray(sa[:, s0:s1].transpose(0, 2, 1, 3))
        sbc = np.concatenate([sbb[:, s0:s1].transpose(0, 2, 1, 3).reshape(NLAYERS, 48, D),
                              sl[:, s0:s1],
                              sc[:, s0:s1].transpose(0, 2, 1, 3).reshape(NLAYERS, 32, D)], axis=1)
        m = dict(shared)
        m["xin"] = np.ascontiguousarray(xin)
        m["sa_in"] = sa_in
        m["sbc_in"] = np.ascontiguousarray(sbc)
        maps.append(m)
    return maps


def unpack_outputs(results, nl=NLAYERS):
    y_p = np.zeros((NCORES, 2048, D), np.float32)
    y_s = np.zeros((NCORES * NSEQ, 4, D), np.float32)
    pa = np.zeros((NLAYERS, NCORES, 30, D), np.float32)
    pb = np.zeros((NLAYERS, NCORES, 3, D), np.float32)
    ph = np.zeros((NLAYERS, NCORES, D), np.float32)
    pc = np.zeros((NLAYERS, NCORES, 2, D), np.float32)
    sa = np.zeros((NLAYERS, NCORES * NSEQ, 30, D), np.float32)
    sb_ = np.zeros((NLAYERS, NCORES * NSEQ, 3, D), np.float32)
    sh = np.zeros((NLAYERS, NCORES * NSEQ, D), np.float32)
    sc = np.zeros((NLAYERS, NCORES * NSEQ, 2, D), np.float32)
    for b in range(NCORES):
        r = results[b]
        s0, s1 = b * NSEQ, (b + 1) * NSEQ
        y = np.asarray(r["yout"])
        y_p[b] = y[:2048]
        y_s[s0:s1] = y[2048:].reshape(NSEQ, 4, D)
        st = np.asarray(r["st_out"])
        old = np.asarray(r["sa_old"])
        pa[:, b] = st[:, 0:30]
        pb[:, b] = st[:, 30:33]
        ph[:, b] = st[:, 33]
        pc[:, b] = st[:, 34:36]
        sa[:, s0:s1, 0:26] = old.transpose(0, 2, 1, 3)
        sa[:, s0:s1, 26:30] = st[:, 36:100].reshape(NLAYERS, NSEQ, 4, D)
        sb_[:, s0:s1] = st[:, 100:148].reshape(NLAYERS, NSEQ, 3, D)
        sh[:, s0:s1] = st[:, 148:164]
        sc[:, s0:s1] = st[:, 164:196].reshape(NLAYERS, NSEQ, 2, D)
    return (y_p, y_s, pa, pb, ph, pc, sa, sb_, sh, sc)


def kernel(**inputs):
    maps = pack_inputs(inputs)
    nc, _ = build_program()
    res = run_bass_kernel_spmd(nc, maps, core_ids=list(range(NCORES)))
    return unpack_outputs(res.results)
```

```python
import contextlib
import numpy as np
import concourse.bass as bass
import concourse.mybir as mybir
from concourse.bass_utils import run_bass_kernel_spmd

F32 = mybir.dt.float32
BF16 = mybir.dt.bfloat16
AF = mybir.ActivationFunctionType
ALU = mybir.AluOpType

NCORES = 8
D = 1024
NCH = 8
IN_COLS = 12288
NSEQ = 16
EPS = 1e-6
NLAYERS = 4
TILES = [(0, 512, 'p'), (512, 512, 'p'), (1024, 512, 'p'), (1536, 512, 'p'), (2048, 64, 's')]
PASSES = [[0], [1], [2], [3, 4]]
TPB = 576
T_ALL = 2112
NVEC = 59
NSOUT = 196
NRING = 6
DG_ENG = "dve"
GC = 256

V_NPRE, V_NPOST, V_BIN, V_CAW, V_CAB, V_LNG, V_LNB, V_CBW, V_CBB, V_RBA, V_RBX, V_LAM, V_CCW = \
    0, 1, 2, 14, 45, 46, 47, 48, 52, 53, 54, 55, 56
C_AV, C_AG, C_AZ, C_BX, C_BZ, C_CB, C_CC, C_CX, C_CZ, C_G = \
    0, 1024, 2048, 3072, 4096, 5120, 6144, 7168, 8192, 9216


class Ins:
    __slots__ = ("eng", "fn", "deps", "idx", "inc", "val", "is_dma", "sem", "semval")


class Tracker:
    ENGS = ("pe", "act", "dve", "pool", "sp")

    def __init__(self):
        self.lists = {e: [] for e in self.ENGS}
        self.lastw = {}
        self.readers = {}
        self.dma_sems = {}

    def op(self, eng, fn, reads=(), writes=(), dma_sem=None):
        I = Ins()
        I.eng = eng
        I.fn = fn
        I.idx = len(self.lists[eng])
        I.inc = False
        I.val = 0
        I.is_dma = dma_sem is not None
        I.sem = dma_sem
        I.semval = 0
        deps = set()
        for r in reads:
            w = self.lastw.get(r)
            if w is not None:
                deps.add(w)
        for k in writes:
            w = self.lastw.get(k)
            if w is not None:
                deps.add(w)
            rd = self.readers.get(k)
            if rd:
                deps.update(rd.values())
        if dma_sem is not None:
            lst = self.dma_sems.setdefault(dma_sem, [])
            if lst:
                deps.add(lst[-1])
            I.semval = 16 * (len(lst) + 1)
            lst.append(I)
        deps.discard(I)
        I.deps = deps
        for d in deps:
            if not d.is_dma:
                d.inc = True
        for r in reads:
            rd = self.readers.setdefault(r, {})
            rd[("d", id(I)) if I.is_dma else eng] = I
        for k in writes:
            self.lastw[k] = I
            self.readers[k] = {}
        self.lists[eng].append(I)
        return I


def build_program(nl=NLAYERS, passes=PASSES, debug=False):
    nc = bass.Bass("TRN2", target_bir_lowering=False)
    T = Tracker()
    dbg_list = []
    marks = []

    def MARK(label):
        marks.append((label, len(T.lists['pe']), len(T.lists['act']), len(T.lists['dve'])))

    def DBG(name, ap, shape, dt, keys):
        if not debug:
            return
        dd = nc.dram_tensor("dbg_" + name, list(shape), dt, kind="ExternalOutput").ap()
        dbg_list.append(name)
        T.op("sp", lambda e: e.dma_start(out=dd, in_=ap), keys, [], dma_sem=dsem("dbg_" + name))

    def din(name, shape):
        return nc.dram_tensor(name, list(shape), F32, kind="ExternalInput").ap()

    def dout(name, shape):
        return nc.dram_tensor(name, list(shape), F32, kind="ExternalOutput").ap()

    xin = din("xin", [T_ALL, D])
    sa_in = din("sa_in", [NLAYERS, 30, NSEQ, D])
    sbc_in = din("sbc_in", [NLAYERS, 96, D])
    vecs = din("vecs", [256, D])
    ident_d = din("ident", [128, 128])
    w_in = din("w_in", [NLAYERS, D, IN_COLS])
    w_a_out = din("w_a_out", [NLAYERS, D, D])
    w_b_out = din("w_b_out", [NLAYERS, D, D])
    w_c_out = din("w_c_out", [NLAYERS, D, D])
    w_o = din("w_o", [NLAYERS, D, D])
    rg_w_a = din("rg_w_a", [NLAYERS, 16, 64, 64])
    rg_w_x = din("rg_w_x", [NLAYERS, 16, 64, 64])
    yout = dout("yout", [T_ALL, D])
    st_out = dout("st_out", [NLAYERS, NSOUT, D])
    sa_old = dout("sa_old", [NLAYERS, 26, NSEQ, D])
    dgs = nc.dram_tensor("dgs", [NLAYERS, NCH, 128, 31 * 128], BF16, kind="Internal").ap()

    es = contextlib.ExitStack()

    def sb(name, shape, dt):
        return es.enter_context(nc.sbuf_tensor(name, list(shape), dt))

    X = sb("X", [128, NCH, TPB], F32)
    H = sb("H", [128, NCH, TPB], BF16)
    R32 = sb("R32", [128, NCH * TPB], F32)
    PC = sb("PC", [128, NCH, TPB], BF16)
    M = sb("M", [128, NCH, TPB], BF16)
    RING = sb("RING", [128, NRING, NCH, GC], BF16)
    DG = sb("DG", [128, 2, 31, 128], BF16)
    NTF = 19
    NTB = 7
    TF = sb("TF", [128, NTF, 516], F32)
    TB = sb("TB", [128, NTB, 576], BF16)
    VT = sb("VT", [128, NCH, 256], F32)
    CL = sb("CL", [128, NLAYERS, NCH], F32)
    CLT = sb("CLT", [128, NLAYERS, NCH], F32)
    CL2 = sb("CL2", [128, NLAYERS, NCH], F32)
    IDF = sb("IDF", [128, 128], F32)
    IDB = sb("IDB", [128, 128], BF16)
    ONESB = sb("ONESB", [128, 128], BF16)
    RG = sb("RG", [128, NLAYERS, NCH, 2, 128], BF16)
    ORW = sb("ORW", [128, 2, D], F32)
    SOUT = sb("SOUT", [128, NCH, NSOUT], F32)
    US = sb("US", [128, NCH, NSEQ, 34], BF16)
    BXS = sb("BXS", [128, NCH, NSEQ, 7], F32)
    CINS = sb("CINS", [128, NCH, NSEQ, 6], F32)
    LRUS = sb("LRUS", [128, NCH, NSEQ], F32)
    HA = sb("HA", [128, NLAYERS, NCH, 30], BF16)
    HB = sb("HB", [128, NLAYERS, NCH, 3], F32)
    HC = sb("HC", [128, NLAYERS, NCH, 2], F32)
    HL = sb("HL", [128, NLAYERS, NCH], F32)
    PS = es.enter_context(nc.psum_tensor("PS", [128, 8, 512], F32))

    Rb = R32[:].bitcast(BF16).rearrange("p (c t) -> p c t", t=TPB)
    O = R32[:].rearrange("p (c t) -> p c t", t=TPB)

    sems = {}
    for e in Tracker.ENGS:
        sems[("eng", e)] = es.enter_context(nc.semaphore("s_" + e))

    def dsem(name):
        k = ("dma", name)
        if k not in sems:
            sems[k] = es.enter_context(nc.semaphore("d_" + str(name)))
        return name

    def kw_act(bias, scale):
        kw = {}
        if bias is not None:
            kw["bias"] = bias
        if scale is not None:
            kw["scale"] = scale
        return kw

    def ACT(out, in_, func, reads, writes, bias=None, scale=None):
        kw = kw_act(bias, scale)
        return T.op("act", lambda e: e.activation(out=out, in_=in_, func=func, **kw), reads, writes)

    def TT(out, in0, in1, op, reads, writes, eng="dve"):
        return T.op(eng, lambda e: e.tensor_tensor(out, in0, in1, op), reads, writes)

    def STT(out, in0, scalar, in1, op0, op1, reads, writes, eng="dve"):
        return T.op(eng, lambda e: e.scalar_tensor_tensor(out, in0, scalar, in1, op0, op1), reads, writes)

    def TS(out, in0, s1, s2, op0, op1, reads, writes, eng="dve"):
        if s2 is None:
            return T.op(eng, lambda e: e.tensor_scalar(out, in0, s1, None, op0), reads, writes)
        return T.op(eng, lambda e: e.tensor_scalar(out, in0, s1, s2, op0, op1), reads, writes)

    def CP(out, in_, reads, writes, eng="dve"):
        if eng == "act":
            return T.op(eng, lambda e: e.copy(out, in_), reads, writes)
        return T.op(eng, lambda e: e.tensor_copy(out, in_), reads, writes)

    def MM(out, lhsT, rhs, start, stop, reads, writes):
        return T.op("pe", lambda e: e.matmul(out, lhsT, rhs, start=start, stop=stop), reads, writes)

    def TR(out, in_, ident, reads, writes):
        return T.op("pe", lambda e: e.transpose(out, in_, ident), reads, writes)

    def DMA(eng, out, in_, semname, reads, writes, **kw):
        dsem(semname)
        return T.op(eng, lambda e: e.dma_start(out=out, in_=in_, **kw), reads, writes, dma_sem=semname)

    misc_rr = [0]

    def misc_sem():
        misc_rr[0] = (misc_rr[0] + 1) % 8
        return "m%d" % misc_rr[0]

    bank_free = list(range(8))

    def bank():
        if not bank_free:
            raise RuntimeError("PSUM bank liveness exceeded 8")
        return bank_free.pop(0)

    def bfree(*bs):
        for b in bs:
            assert b not in bank_free
            bank_free.append(b)

    def hold_bank():
        return bank()

    def unhold(b):
        pass

    def psk(b):
        return ("ps", b)

    def tf(i, n, off=0):
        return TF[:, i, off:off + n]

    def tfk(i):
        return ("tf", i)

    def tb(i, n):
        return TB[:, i, 0:n]

    def tbk(i):
        return ("tb", i)

    def vt(l, row, c):
        return VT[:, c, l * NVEC + row: l * NVEC + row + 1]

    def b_in(l, colbase, c):
        j = colbase // 128 + c
        return VT[:, j % 8, l * NVEC + V_BIN + j // 8: l * NVEC + V_BIN + j // 8 + 1]

    def wsrc(name, l, col0):
        if name == "w_in":
            v = w_in[l].rearrange("(k p) c -> p k c", p=128)
        else:
            v = {"w_a_out": w_a_out, "w_b_out": w_b_out, "w_c_out": w_c_out, "w_o": w_o}[name][l] \
                .rearrange("(k p) c -> p k c", p=128)
        return v[:, :, col0:col0 + GC]

    ws = {"free": list(range(NRING)), "slot_of": {}}

    def ws_get(p_i, l, name, col0):
        key = (p_i, l, name, col0)
        if key not in ws["slot_of"]:
            slot = ws["free"].pop(0)
            ws["slot_of"][key] = slot
            DMA("pool", RING[:, slot], wsrc(name, l, col0), "wr%d" % slot, [], [("wr", slot)])
        return ws["slot_of"][key]

    def ws_release(p_i, l, name, col0):
        slot = ws["slot_of"].pop((p_i, l, name, col0))
        ws["free"].append(slot)

    DMA("sp", IDF[:], ident_d, misc_sem(), [], [("IDF",)])
    CP(IDB[:], IDF[:], [("IDF",)], [("IDB",)])
    T.op("dve", lambda e: e.memset(ONESB[:], 1.0), [], [("ONESB",)])
    T.op("dve", lambda e: e.memset(HA[:], 0.0), [], [("HA",)])
    T.op("dve", lambda e: e.memset(HB[:], 0.0), [], [("HB",)])
    T.op("dve", lambda e: e.memset(HC[:], 0.0), [], [("HC",)])
    T.op("dve", lambda e: e.memset(HL[:], 0.0), [], [("HL",)])
    T.op("dve", lambda e: e.memset(RG[:], 0.0), [], [("RG",)])

    for rt in range(2):
        DMA("sp", ORW[:, rt, :], vecs[rt * 128:(rt + 1) * 128, :], misc_sem(), [], [("ORW", rt)])
        for half in range(2):
            b = bank()
            for j in range(4):
                c = half * 4 + j
                TR(PS[:, b, j * 128:(j + 1) * 128], ORW[:, rt, c * 128:(c + 1) * 128], IDF[:],
                   [("ORW", rt), ("IDF",)], [psk(b)])
            ACT(VT[:, half * 4:half * 4 + 4, rt * 128:(rt + 1) * 128],
                PS[:, b, :].rearrange("p (j r) -> p j r", r=128), AF.Identity,
                [psk(b)], [("VT",)])
            bfree(b)
    for l in range(NLAYERS):
        lam = VT[:, :, l * NVEC + V_LAM]
        ACT(CLT[:, l, :], lam, AF.Exp, [("VT",)], [("CLT",)], scale=-1.0)
    ACT(CLT[:], CLT[:], AF.Ln, [("CLT",)], [("CLT",)], bias=1.0)
    TS(CL[:], CLT[:], -8.0, None, ALU.mult, None, [("CLT",)], [("CL",)])
    TS(CL2[:], CLT[:], -16.0, None, ALU.mult, None, [("CLT",)], [("CL",)])

    def rg_load(l):
        for gi, rw in enumerate((rg_w_a, rg_w_x)):
            for half in range(2):
                src = rw[l].rearrange("(c h) i j -> h i c j", h=2)[half]
                DMA("pool", RG[half * 64:(half + 1) * 64, l, :, gi, half * 64:(half + 1) * 64], src,
                    "rg%d" % (gi * 2 + half), [("RG",)], [("RG", l, gi, half)])

    def run_pipeline(gens):
        active = []
        for g in gens:
            active.append(g)
            for a in list(active):
                try:
                    next(a)
                except StopIteration:
                    active.remove(a)
        while active:
            for a in list(active):
                try:
                    next(a)
                except StopIteration:
                    active.remove(a)

    def proj(p_i, l, name, col0, cc, rhs_buf, rhs_keyf, off, nt):
        slot = ws_get(p_i, l, name, col0)
        b = bank()
        for k in range(NCH):
            MM(PS[:, b, 0:nt], RING[:, slot, k, cc * 128:(cc + 1) * 128], rhs_buf[:, k, off:off + nt],
               k == 0, k == NCH - 1, [("wr", slot), rhs_keyf(k)], [psk(b)])
        return b

    def proj_multi(p_i, l, name, col0, cc, rhs_buf, rhs_keyf, utiles):
        slot = ws_get(p_i, l, name, col0)
        banks = {t[0]: bank() for t in utiles}
        for k in range(NCH):
            for (ti, st, nt, kind, off) in utiles:
                MM(PS[:, banks[ti], 0:nt], RING[:, slot, k, cc * 128:(cc + 1) * 128], rhs_buf[:, k, off:off + nt],
                   k == 0, k == NCH - 1, [("wr", slot), rhs_keyf(k)], [psk(banks[ti])])
        return banks

    def stats_rs(b, nt, out_i):
        ACT(tf(out_i, nt), PS[:, b, 0:nt], AF.Ln, [psk(b)], [tfk(out_i)], bias=EPS, scale=1.0 / D)
        ACT(tf(out_i, nt), tf(out_i, nt), AF.Exp, [tfk(out_i)], [tfk(out_i)], scale=-0.5)

    for p_i, pss in enumerate(passes):
        tiles = []
        base = TILES[pss[0]][0]
        for ti in pss:
            st, nt, kind = TILES[ti]
            tiles.append((ti, st, nt, kind, st - base))
        is_last_pass = (pss[-1] == len(TILES) - 1)

        def Xk(c):
            return ("X", c)

        def Hk(c):
            return ("H", c)

        def Rk(j):
            return ("R", j)

        def PCk(c):
            return ("PC", c)

        def Mk(c):
            return ("M", c)

        for (ti, st, nt, kind, off) in tiles:
            for r0 in range(0, nt, 128):
                nr = min(128, nt - r0)
                slot = (r0 // 128) % 2
                DMA("sp", ORW[0:nr, slot, :], xin[st + r0: st + r0 + nr, :], misc_sem(), [], [("ORW", slot)])
                for half in range(2):
                    b = bank()
                    for j in range(4):
                        c = half * 4 + j
                        TR(PS[:, b, j * 128: j * 128 + nr], ORW[0:nr, slot, c * 128:(c + 1) * 128],
                           IDF[0:nr, 0:nr], [("ORW", slot), ("IDF",)], [psk(b)])
                    ACT(X[:, half * 4:half * 4 + 4, off + r0: off + r0 + nr],
                        PS[:, b, :].rearrange("p (j r) -> p j r", r=128)[:, :, 0:nr], AF.Identity,
                        [psk(b)], [Xk(half * 4 + j) for j in range(4)])
                    bfree(b)

        if p_i == 0:
            for l in range(nl):
                DMA("sp", sa_old[l], sa_in[l, 4:30], misc_sem(), [], [])

        for l in range(nl):
            VB = l * NVEC
            if p_i == 0:
                rg_load(l)
            if is_last_pass:
                for i in range(4):
                    nr = 128 if i < 3 else 96
                    nj = nr // 16
                    slot = i % 2
                    DMA("sp", ORW[0:nr, slot, :],
                        sa_in[l].rearrange("j s d -> (j s) d")[i * 128: i * 128 + nr, :],
                        misc_sem(), [], [("ORW", slot)])
                    for half in range(2):
                        b = bank()
                        for j in range(4):
                            c = half * 4 + j
                            TR(PS[:, b, j * 128: j * 128 + nr], ORW[0:nr, slot, c * 128:(c + 1) * 128],
                               IDF[0:nr, 0:nr], [("ORW", slot), ("IDF",)], [psk(b)])
                        for j in range(4):
                            c = half * 4 + j
                            ACT(US[:, c, :, i * 8: i * 8 + nj],
                                PS[:, b, j * 128: j * 128 + nr].rearrange("p (j s) -> p s j", s=NSEQ),
                                AF.Identity, [psk(b)], [("US", c)])
                        bfree(b)
                slot = 0
                DMA("sp", ORW[0:96, slot, :], sbc_in[l], misc_sem(), [], [("ORW", slot)])
                for half in range(2):
                    b = bank()
                    for j in range(4):
                        c = half * 4 + j
                        TR(PS[:, b, j * 128: j * 128 + 96], ORW[0:96, slot, c * 128:(c + 1) * 128],
                           IDF[0:96, 0:96], [("ORW", slot), ("IDF",)], [psk(b)])
                    for j in range(4):
                        c = half * 4 + j
                        ACT(BXS[:, c, :, 0:3], PS[:, b, j * 128: j * 128 + 48].rearrange("p (j s) -> p s j", s=NSEQ),
                            AF.Identity, [psk(b)], [("BXS", c)])
                        ACT(LRUS[:, c, :], PS[:, b, j * 128 + 48: j * 128 + 64],
                            AF.Identity, [psk(b)], [("LRUS", c)])
                        ACT(CINS[:, c, :, 0:2], PS[:, b, j * 128 + 64: j * 128 + 96].rearrange("p (j s) -> p s j", s=NSEQ),
                            AF.Identity, [psk(b)], [("CINS", c)])
                    bfree(b)

            MARK(('PRE', p_i, l if 'PRE' != 'Y' else -1))
            for (ti, st, nt, kind, off) in tiles:
                bs = hold_bank()
                for c in range(NCH):
                    i = c % 2
                    ACT(tb(i, nt), X[:, c, off:off + nt], AF.Square, [Xk(c)], [tbk(i)])
                    MM(PS[:, bs, 0:nt], ONESB[:], tb(i, nt), c == 0, c == NCH - 1, [tbk(i), ("ONESB",)], [psk(bs)])
                stats_rs(bs, nt, 0)
                bfree(bs)
                for c in range(NCH):
                    STT(H[:, c, off:off + nt], X[:, c, off:off + nt], vt(l, V_NPRE, c), tf(0, nt),
                        ALU.mult, ALU.mult, [Xk(c), tfk(0), ("VT",)], [Hk(c)])

            if p_i == 0 and l == 0:
                DBG("H", H[:], [128, NCH, TPB], BF16, [Hk(c) for c in range(NCH)])
            MARK(('A1', p_i, l if 'A1' != 'Y' else -1))
            def a1_unit(cs, utiles, par):
                q = cs[0] // 2
                items = [(c, t) for c in cs for t in utiles]
                rel = (cs[-1] % 2 == 1) and (utiles[-1] is tiles[-1])
                bv = {}
                bg = {}
                for c in cs:
                    bv[c] = proj_multi(p_i, l, "w_in", C_AV + q * GC, c % 2, H, Hk, utiles)
                    bg[c] = proj_multi(p_i, l, "w_in", C_AG + q * GC, c % 2, H, Hk, utiles)
                if rel:
                    ws_release(p_i, l, "w_in", C_AV + q * GC)
                    ws_release(p_i, l, "w_in", C_AG + q * GC)
                yield
                for j, (c, (ti, st, nt, kind, off)) in enumerate(items):
                    si = par * 2 + j
                    ACT(tf(si, nt), PS[:, bg[c][ti], 0:nt], AF.Sigmoid, [psk(bg[c][ti])], [tfk(si)],
                        bias=b_in(l, C_AG, c))
                for j, (c, (ti, st, nt, kind, off)) in enumerate(items):
                    si = par * 2 + j
                    bvb = bv[c][ti]
                    last_prompt = (kind == 'p' and ti == 3)
                    if kind == 'p':
                        ui = par * 2 + j
                        CP(TB[:, ui, 0:30], HA[:, l, c, :], [("HA", l, c)], [tbk(ui)], eng="act")
                        STT(TB[:, ui, 30:30 + nt], PS[:, bvb, 0:nt], b_in(l, C_AV, c), tf(si, nt),
                            ALU.add, ALU.mult, [psk(bvb), tfk(si), ("VT",)], [tbk(ui)])
                        if last_prompt:
                            STT(SOUT[:, c, 0:30], PS[:, bvb, nt - 30:nt], b_in(l, C_AV, c), tf(si, 30, nt - 30),
                                ALU.add, ALU.mult, [psk(bvb), tfk(si), ("VT",)], [("SOUT", c)])
                    else:
                        STT(US[:, c, :, 30:34], PS[:, bvb, 0:nt].rearrange("p (s t) -> p s t", t=4),
                            b_in(l, C_AV, c), tf(si, nt).rearrange("p (s t) -> p s t", t=4),
                            ALU.add, ALU.mult, [psk(bvb), tfk(si), ("VT",)], [("US", c)])
                        STT(SOUT[:, c, 36:100].rearrange("p (s t) -> p s t", t=4),
                            PS[:, bvb, 0:nt].rearrange("p (s t) -> p s t", t=4),
                            b_in(l, C_AV, c), tf(si, nt).rearrange("p (s t) -> p s t", t=4),
                            ALU.add, ALU.mult, [psk(bvb), tfk(si), ("VT",)], [("SOUT", c)])
                    bfree(bvb, bg[c][ti])
                yield
                for c in cs:
                    di = c % 2
                    if utiles[0] is tiles[0]:
                        if p_i == 0 and l == 0:
                            TT(DG[:, di], IDB[:].unsqueeze(1).to_broadcast([128, 31, 128]),
                               VT[:, c, VB + V_CAW: VB + V_CAW + 31].unsqueeze(2).to_broadcast([128, 31, 128]),
                               ALU.mult, [("IDB",), ("VT",)], [("DG", di)], eng=DG_ENG)
                            if len(passes) > 1:
                                DMA("sp", dgs[l, c], DG[:, di].rearrange("p k m -> p (k m)"), "dgst%d" % di,
                                    [("DG", di)], [("dgs", l, c)])
                        else:
                            DMA("sp", DG[:, di].rearrange("p k m -> p (k m)"), dgs[l, c], "dgld%d" % di,
                                [("dgs", l, c)], [("DG", di)])
                for j, (c, (ti, st, nt, kind, off)) in enumerate(items):
                    if kind == 'p' and ti != 3:
                        ui = par * 2 + j
                        CP(HA[:, l, c, :], TB[:, ui, nt:nt + 30], [tbk(ui)], [("HA", l, c)], eng="act")
                yield
                bc = {}
                for c in cs:
                    di = c % 2
                    its = [(j, it) for j, it in enumerate(items) if it[0] == c]
                    for j, (c_, (ti, st, nt, kind, off)) in its:
                        bc[(c, ti)] = bank()
                    for k in range(31):
                        for j, (c_, (ti, st, nt, kind, off)) in its:
                            if kind == 'p':
                                rhs = TB[:, par * 2 + j, k:k + nt]
                                rk = tbk(par * 2 + j)
                            else:
                                rhs = US[:, c, :, k:k + 4]
                                rk = ("US", c)
                            MM(PS[:, bc[(c, ti)], 0:nt], DG[:, di, k, :], rhs, k == 0, k == 30,
                               [("DG", di), rk], [psk(bc[(c, ti)])])
                yield
                for j, (c, (ti, st, nt, kind, off)) in enumerate(items):
                    ACT(Rb[:, c, off:off + nt], PS[:, bc[(c, ti)], 0:nt], AF.Identity, [psk(bc[(c, ti)])], [Rk(c)],
                        bias=vt(l, V_CAB, c))
                    bfree(bc[(c, ti)])

            def b_unit(c, tile, par):
                ti, st, nt, kind, off = tile
                last_prompt = (kind == 'p' and ti == 3)
                q, cc = c // 2, c % 2
                bi = 4 + par * 5
                xi = bi + 1
                ri = bi + 2
                ai = bi + 3
                zi = bi + 4
                xbb = 4 + par
                bx = proj(p_i, l, "w_in", C_BX + q * GC, cc, H, Hk, off, nt)
                if tile is tiles[-1] and cc == 1:
                    ws_release(p_i, l, "w_in", C_BX + q * GC)
                yield
                if kind == 'p':
                    CP(TF[:, bi, 0:3], HB[:, l, c, :], [("HB", l, c)], [tfk(bi)], eng="act")
                    ACT(TF[:, bi, 3:3 + nt], PS[:, bx, 0:nt], AF.Identity, [psk(bx), ("VT",)], [tfk(bi)],
                        bias=b_in(l, C_BX, c))
                    taps = [TF[:, bi, k:k + nt] for k in range(4)]
                    xb = tf(xi, nt)
                    rk = tfk(bi)
                else:
                    ACT(BXS[:, c, :, 3:7], PS[:, bx, 0:nt].rearrange("p (s t) -> p s t", t=4), AF.Identity,
                        [psk(bx), ("VT",)], [("BXS", c)], bias=b_in(l, C_BX, c))
                    taps = [BXS[:, c, :, k:k + 4] for k in range(4)]
                    xb = tf(xi, nt).rearrange("p (s t) -> p s t", t=4)
                    rk = ("BXS", c)
                bfree(bx)
                TS(xb, taps[3], vt(l, V_CBW + 3, c), vt(l, V_CBB, c), ALU.mult, ALU.add,
                   [rk, ("VT",)], [tfk(xi)])
                for k in (2, 1, 0):
                    STT(xb, taps[k], vt(l, V_CBW + k, c), xb, ALU.mult, ALU.add,
                        [rk, tfk(xi), ("VT",)], [tfk(xi)])
                yield
                CP(tb(xbb, nt), tf(xi, nt), [tfk(xi)], [tbk(xbb)], eng="act")
                if kind == 'p':
                    if last_prompt:
                        CP(SOUT[:, c, 30:33], TF[:, bi, nt:nt + 3], [tfk(bi)], [("SOUT", c)], eng="act")
                    else:
                        CP(HB[:, l, c, :], TF[:, bi, nt:nt + 3], [tfk(bi)], [("HB", l, c)], eng="act")
                else:
                    CP(SOUT[:, c, 100:148].rearrange("p (s j) -> p s j", j=3), BXS[:, c, :, 4:7],
                       [("BXS", c)], [("SOUT", c)], eng="act")
                yield
                br = bank()
                MM(PS[:, br, 0:nt], RG[:, l, c, 0, :], tb(xbb, nt), True, True,
                   [("RG", l, 0, 0), ("RG", l, 0, 1), tbk(xbb)], [psk(br)])
                bi_ = bank()
                MM(PS[:, bi_, 0:nt], RG[:, l, c, 1, :], tb(xbb, nt), True, True,
                   [("RG", l, 1, 0), ("RG", l, 1, 1), tbk(xbb)], [psk(bi_)])
                bz = proj(p_i, l, "w_in", C_BZ + q * GC, cc, H, Hk, off, nt)
                if tile is tiles[-1] and cc == 1:
                    ws_release(p_i, l, "w_in", C_BZ + q * GC)
                yield
                ACT(tf(ri, nt), PS[:, br, 0:nt], AF.Sigmoid, [psk(br), ("VT",)], [tfk(ri)], bias=vt(l, V_RBA, c))
                ACT(tf(bi, nt), PS[:, bi_, 0:nt], AF.Sigmoid, [psk(bi_), ("VT",)], [tfk(bi)], bias=vt(l, V_RBX, c))
                ACT(tf(zi, nt), PS[:, bz, 0:nt], AF.Sigmoid, [psk(bz), ("VT",)], [tfk(zi)], bias=b_in(l, C_BZ, c))
                ACT(tf(ai, nt), tf(ri, nt), AF.Exp, [tfk(ri), ("CL",)], [tfk(ai)], scale=CL[:, l, c:c + 1])
                ACT(tf(ri, nt), tf(ri, nt), AF.Exp, [tfk(ri), ("CL",)], [tfk(ri)], scale=CL2[:, l, c:c + 1])
                ACT(tf(ri, nt), tf(ri, nt), AF.Ln, [tfk(ri)], [tfk(ri)], bias=1.0, scale=-1.0)
                ACT(tf(ri, nt), tf(ri, nt), AF.Exp, [tfk(ri)], [tfk(ri)], scale=0.5)
                bfree(br, bi_)
                TT(tf(bi, nt), tf(bi, nt), tf(xi, nt), ALU.mult, [tfk(bi), tfk(xi)], [tfk(bi)])
                TT(tf(ri, nt), tf(ri, nt), tf(bi, nt), ALU.mult, [tfk(ri), tfk(bi)], [tfk(ri)])
                if kind == 'p':
                    T.op("dve", lambda e, o=tf(xi, nt), a=tf(ai, nt), u=tf(ri, nt), h0=HL[:, l, c:c + 1]:
                         e.tensor_tensor_scan(o, a, u, h0, ALU.mult, ALU.add),
                         [tfk(ai), tfk(ri), ("HL", l, c)], [tfk(xi)])
                    if last_prompt:
                        CP(SOUT[:, c, 33:34], TF[:, xi, nt - 1:nt], [tfk(xi)], [("SOUT", c)], eng="dve")
                    else:
                        CP(HL[:, l, c:c + 1], TF[:, xi, nt - 1:nt], [tfk(xi)], [("HL", l, c)], eng="dve")
                    hs_keys = [tfk(xi)]
                else:
                    for s_ in range(NSEQ):
                        T.op("dve", lambda e, o=TF[:, xi, s_ * 4:s_ * 4 + 4], a=TF[:, ai, s_ * 4:s_ * 4 + 4],
                             u=TF[:, ri, s_ * 4:s_ * 4 + 4], h0=LRUS[:, c, s_:s_ + 1]:
                             e.tensor_tensor_scan(o, a, u, h0, ALU.mult, ALU.add),
                             [tfk(ai), tfk(ri), ("LRUS", c)], [("sub", xi, s_)])
                    hs_keys = [tfk(xi)] + [("sub", xi, s_) for s_ in range(NSEQ)]
                    CP(SOUT[:, c, 148:164], tf(xi, nt).rearrange("p (s t) -> p s t", t=4)[:, :, 3],
                       hs_keys, [("SOUT", c)], eng="dve")
                STT(tf(xi, nt), PS[:, bz, 0:nt], b_in(l, C_BZ, c), tf(xi, nt), ALU.add, ALU.mult,
                    hs_keys + [psk(bz), ("VT",)], [tfk(xi)])
                bfree(bz)
                TT(Rb[:, 8 + c, off:off + nt], tf(xi, nt), tf(zi, nt), ALU.mult, [tfk(xi), tfk(zi)], [Rk(8 + c)])

            def unit_groups(q):
                if len(tiles) == 1:
                    return [((2 * q, 2 * q + 1), tiles)]
                return [((2 * q,), tiles), ((2 * q + 1,), tiles)]

            units = []
            nA = 0
            nB = 0

            for q in range(5):
                if q >= 1:
                    for (cs_, ut_) in unit_groups(q - 1):
                        units.append(a1_unit(cs_, ut_, nA % 2))
                        nA += 1
                if q < 4:
                    for c in (2 * q, 2 * q + 1):
                        for tile in tiles:
                            units.append(b_unit(c, tile, nB % 3))
                            nB += 1
            run_pipeline(units)

            if p_i == 0 and l == 0:
                DBG("V", R32[:], [128, NCH * TPB], F32, [Rk(c) for c in range(16)])
            MARK(('A2', p_i, l if 'A2' != 'Y' else -1))
            ln_slots = {}
            for tix, (ti, st, nt, kind, off) in enumerate(tiles):
                b1 = hold_bank()
                b2 = hold_bank()
                for c in range(NCH):
                    i = c % 2
                    ACT(tb(i, nt), Rb[:, c, off:off + nt], AF.Square, [Rk(c)], [tbk(i)])
                    MM(PS[:, b1, 0:nt], ONESB[:], Rb[:, c, off:off + nt], c == 0, c == NCH - 1,
                       [Rk(c), ("ONESB",)], [psk(b1)])
                    MM(PS[:, b2, 0:nt], ONESB[:], tb(i, nt), c == 0, c == NCH - 1,
                       [tbk(i), ("ONESB",)], [psk(b2)])
                mu_i = 10 + 2 * tix
                rs_i = 11 + 2 * tix
                ln_slots[ti] = (mu_i, rs_i)
                T.op("act", lambda e, o=tf(mu_i, nt), i_=PS[:, b1, 0:nt]: e.mul(o, i_, 1.0 / D),
                     [psk(b1)], [tfk(mu_i)])
                TT(tf(rs_i, nt), tf(mu_i, nt), tf(mu_i, nt), ALU.mult, [tfk(mu_i)], [tfk(rs_i)])
                STT(tf(rs_i, nt), PS[:, b2, 0:nt], 1.0 / D, tf(rs_i, nt), ALU.mult, ALU.subtract,
                    [psk(b2), tfk(rs_i)], [tfk(rs_i)])
                bfree(b1, b2)
                ACT(tf(rs_i, nt), tf(rs_i, nt), AF.Ln, [tfk(rs_i)], [tfk(rs_i)], bias=EPS)
                ACT(tf(rs_i, nt), tf(rs_i, nt), AF.Exp, [tfk(rs_i)], [tfk(rs_i)], scale=-0.5)

            MARK(('A3', p_i, l if 'A3' != 'Y' else -1))
            def a3_unit(cs, utiles, par):
                q = cs[0] // 2
                items = [(c, t) for c in cs for t in utiles]
                rel = (cs[-1] % 2 == 1) and (utiles[-1] is tiles[-1])
                bz = {}
                for c in cs:
                    bz[c] = proj_multi(p_i, l, "w_in", C_AZ + q * GC, c % 2, H, Hk, utiles)
                if rel:
                    ws_release(p_i, l, "w_in", C_AZ + q * GC)
                yield
                for j, (c, (ti, st, nt, kind, off)) in enumerate(items):
                    mu_i, rs_i = ln_slots[ti]
                    ni = par * 2 + j
                    TT(tf(ni, nt), Rb[:, c, off:off + nt], tf(mu_i, nt), ALU.subtract, [Rk(c), tfk(mu_i)], [tfk(ni)])
                    TT(tf(ni, nt), tf(ni, nt), tf(rs_i, nt), ALU.mult, [tfk(ni), tfk(rs_i)], [tfk(ni)])
                for j, (c, (ti, st, nt, kind, off)) in enumerate(items):
                    ni = par * 2 + j
                    zi = 4 + par * 2 + j
                    ACT(tf(ni, nt), tf(ni, nt), AF.Silu, [tfk(ni), ("VT",)], [tfk(ni)],
                        bias=vt(l, V_LNB, c), scale=vt(l, V_LNG, c))
                    ACT(tf(zi, nt), PS[:, bz[c][ti], 0:nt], AF.Silu, [psk(bz[c][ti]), ("VT",)], [tfk(zi)],
                        bias=b_in(l, C_AZ, c))
                    bfree(bz[c][ti])
                for j, (c, (ti, st, nt, kind, off)) in enumerate(items):
                    ni = par * 2 + j
                    zi = 4 + par * 2 + j
                    TT(Rb[:, c, off:off + nt], tf(ni, nt), tf(zi, nt), ALU.mult, [tfk(ni), tfk(zi)], [Rk(c)])

            units = []
            n = 0
            for q in range(4):
                for (cs_, ut_) in unit_groups(q):
                    units.append(a3_unit(cs_, ut_, n % 2))
                    n += 1
            run_pipeline(units)

            if p_i == 0 and l == 0:
                DBG("MU", TF[:, 10:12, :], [128, 2, 516], F32, [tfk(10), tfk(11)])
                DBG("PA", R32[:], [128, NCH * TPB], F32, [Rk(c) for c in range(16)])
            MARK(('C', p_i, l if 'C' != 'Y' else -1))
            def c_unit(cs, utiles, par):
                q = cs[0] // 2
                items = [(c, t) for c in cs for t in utiles]
                rel = (cs[-1] % 2 == 1) and (utiles[-1] is tiles[-1])
                bcc = {}
                bcx = {}
                for c in cs:
                    bcc[c] = proj_multi(p_i, l, "w_in", C_CC + q * GC, c % 2, H, Hk, utiles)
                    bcx[c] = proj_multi(p_i, l, "w_in", C_CX + q * GC, c % 2, H, Hk, utiles)
                if rel:
                    ws_release(p_i, l, "w_in", C_CC + q * GC)
                    ws_release(p_i, l, "w_in", C_CX + q * GC)
                yield
                for j, (c, (ti, st, nt, kind, off)) in enumerate(items):
                    last_prompt = (kind == 'p' and ti == 3)
                    ci = par * 2 + j
                    ni = 4 + par * 2 + j
                    xi = 8 + par * 2 + j
                    b1, b2 = bcc[c][ti], bcx[c][ti]
                    ACT(tf(ci, nt), PS[:, b1, 0:nt], AF.Identity, [psk(b1), ("VT",)], [tfk(ci)],
                        bias=b_in(l, C_CC, c))
                    if kind == 'p':
                        CP(TF[:, ni, 0:2], HC[:, l, c, :], [("HC", l, c)], [tfk(ni)], eng="act")
                        STT(TF[:, ni, 2:2 + nt], PS[:, b2, 0:nt], b_in(l, C_CX, c), tf(ci, nt),
                            ALU.add, ALU.mult, [psk(b2), tfk(ci), ("VT",)], [tfk(ni)])
                        if last_prompt:
                            CP(SOUT[:, c, 34:36], TF[:, ni, nt:nt + 2], [tfk(ni)], [("SOUT", c)], eng="dve")
                        else:
                            CP(HC[:, l, c, :], TF[:, ni, nt:nt + 2], [tfk(ni)], [("HC", l, c)], eng="dve")
                        taps = [TF[:, ni, k:k + nt] for k in range(3)]
                        cx = tf(xi, nt)
                        rk = tfk(ni)
                    else:
                        STT(CINS[:, c, :, 2:6], PS[:, b2, 0:nt].rearrange("p (s t) -> p s t", t=4),
                            b_in(l, C_CX, c), tf(ci, nt).rearrange("p (s t) -> p s t", t=4),
                            ALU.add, ALU.mult, [psk(b2), tfk(ci), ("VT",)], [("CINS", c)])
                        CP(SOUT[:, c, 164:196].rearrange("p (s j) -> p s j", j=2), CINS[:, c, :, 4:6],
                           [("CINS", c)], [("SOUT", c)], eng="dve")
                        taps = [CINS[:, c, :, k:k + 4] for k in range(3)]
                        cx = tf(xi, nt).rearrange("p (s t) -> p s t", t=4)
                        rk = ("CINS", c)
                    TS(cx, taps[2], vt(l, V_CCW + 2, c), None, ALU.mult, None, [rk, ("VT",)], [tfk(xi)])
                    for k in (1, 0):
                        STT(cx, taps[k], vt(l, V_CCW + k, c), cx, ALU.mult, ALU.add,
                            [rk, tfk(xi), ("VT",)], [tfk(xi)])
                    bfree(b1, b2)
                yield
                bcb = {}
                bcz = {}
                for c in cs:
                    bcb[c] = proj_multi(p_i, l, "w_in", C_CB + q * GC, c % 2, H, Hk, utiles)
                    bcz[c] = proj_multi(p_i, l, "w_in", C_CZ + q * GC, c % 2, H, Hk, utiles)
                if rel:
                    ws_release(p_i, l, "w_in", C_CB + q * GC)
                    ws_release(p_i, l, "w_in", C_CZ + q * GC)
                yield
                for j, (c, (ti, st, nt, kind, off)) in enumerate(items):
                    ci = par * 2 + j
                    xi = 8 + par * 2 + j
                    b1, b2 = bcb[c][ti], bcz[c][ti]
                    STT(tf(xi, nt), PS[:, b1, 0:nt], b_in(l, C_CB, c), tf(xi, nt), ALU.add, ALU.mult,
                        [psk(b1), tfk(xi), ("VT",)], [tfk(xi)])
                    ACT(tf(ci, nt), PS[:, b2, 0:nt], AF.Silu, [psk(b2), ("VT",), tfk(ci)], [tfk(ci)],
                        bias=b_in(l, C_CZ, c))
                    TT(PC[:, c, off:off + nt], tf(xi, nt), tf(ci, nt), ALU.mult, [tfk(xi), tfk(ci)], [PCk(c)])
                    bfree(b1, b2)

            units = []
            n = 0
            for q in range(4):
                for (cs_, ut_) in unit_groups(q):
                    units.append(c_unit(cs_, ut_, n % 2))
                    n += 1
            run_pipeline(units)

            MARK(('SO', p_i, l if 'SO' != 'Y' else -1))
            if is_last_pass:
                for c in range(NCH):
                    for (c0, ncol, slot) in ((0, 128, 0), (128, NSOUT - 128, 1)):
                        pass
                for (c0, ncol, slot) in ((0, 128, 0), (128, NSOUT - 128, 1)):
                    for half in range(2):
                        b = bank()
                        for j in range(4):
                            c = half * 4 + j
                            TR(PS[0:ncol, b, j * 128:(j + 1) * 128], SOUT[:, c, c0:c0 + ncol], IDF[:],
                               [("SOUT", c), ("IDF",)], [psk(b)])
                        ACT(ORW[0:ncol, slot, half * 512:(half + 1) * 512], PS[0:ncol, b, :], AF.Identity,
                            [psk(b)], [("ORW", slot)])
                        bfree(b)
                    DMA("sp", st_out[l, c0:c0 + ncol, :], ORW[0:ncol, slot, :], misc_sem(), [("ORW", slot)], [])

            if p_i == 0 and l == 0:
                DBG("PB", R32[:], [128, NCH * TPB], F32, [Rk(c) for c in range(16)])
                DBG("PC", PC[:], [128, NCH, TPB], BF16, [PCk(c) for c in range(NCH)])
            MARK(('MERGE', p_i, l if 'MERGE' != 'Y' else -1))
            def m_unit(cs, utiles, par, jb):
                q = cs[0] // 2
                items = [(c, t) for c in cs for t in utiles]
                rel = (cs[-1] % 2 == 1) and (utiles[-1] is tiles[-1])
                srcs = (("w_a_out", lambda k: Rk(k), 0), ("w_b_out", lambda k: Rk(8 + k), 8), ("w_c_out", PCk, None))
                name, keyf, rbase = srcs[jb]
                by = {}
                bgt = {}
                slot = ws_get(p_i, l, name, q * GC)
                for c in cs:
                    by[c] = {t[0]: bank() for t in utiles}
                    for k in range(NCH):
                        for (ti, st, nt, kind, off) in utiles:
                            rhs = Rb[:, rbase + k, off:off + nt] if rbase is not None else PC[:, k, off:off + nt]
                            MM(PS[:, by[c][ti], 0:nt], RING[:, slot, k, (c % 2) * 128:(c % 2 + 1) * 128], rhs,
                               k == 0, k == NCH - 1, [("wr", slot), keyf(k)], [psk(by[c][ti])])
                    bgt[c] = proj_multi(p_i, l, "w_in", C_G + jb * 1024 + q * GC, c % 2, H, Hk, utiles)
                if rel:
                    ws_release(p_i, l, name, q * GC)
                    ws_release(p_i, l, "w_in", C_G + jb * 1024 + q * GC)
                yield
                for j, (c, (ti, st, nt, kind, off)) in enumerate(items):
                    gi = (jb % 2) * 2 + j
                    t2 = 8 + (jb % 2) * 2 + j
                    mt = 4 + par * 2 + j
                    b1, b2 = by[c][ti], bgt[c][ti]
                    ACT(tf(gi, nt), PS[:, b2, 0:nt], AF.Sigmoid, [psk(b2), ("VT",)], [tfk(gi)],
                        bias=b_in(l, C_G + jb * 1024, c))
                    if jb == 0:
                        TT(tf(mt, nt), tf(gi, nt), PS[:, b1, 0:nt], ALU.mult, [tfk(gi), psk(b1)], [tfk(mt)])
                    else:
                        TT(tf(t2, nt), tf(gi, nt), PS[:, b1, 0:nt], ALU.mult, [tfk(gi), psk(b1)], [tfk(t2)])
                        if jb == 1:
                            TT(tf(mt, nt), tf(mt, nt), tf(t2, nt), ALU.add, [tfk(mt), tfk(t2)], [tfk(mt)])
                        else:
                            TT(M[:, c, off:off + nt], tf(mt, nt), tf(t2, nt), ALU.add, [tfk(mt), tfk(t2)], [Mk(c)])
                    bfree(b1, b2)

            def diag_unit(ln, c):
                di = c % 2
                TT(DG[:, di], IDB[:].unsqueeze(1).to_broadcast([128, 31, 128]),
                   VT[:, c, ln * NVEC + V_CAW: ln * NVEC + V_CAW + 31].unsqueeze(2).to_broadcast([128, 31, 128]),
                   ALU.mult, [("IDB",), ("VT",)], [("DG", di)], eng=DG_ENG)
                DMA("sp", dgs[ln, c], DG[:, di].rearrange("p k m -> p (k m)"), "dgst%d" % di,
                    [("DG", di)], [("dgs", ln, c)])
                return
                yield

            units = []
            n = 0
            ndg = 0
            for q in range(4):
                for (cs_, ut_) in unit_groups(q):
                    for jb in range(3):
                        units.append(m_unit(cs_, ut_, n % 2, jb))
                        if p_i == 0 and l + 1 < nl and ndg < NCH:
                            units.append(diag_unit(l + 1, ndg))
                            ndg += 1
                    n += 1
            run_pipeline(units)

            if p_i == 0 and l == 0:
                DBG("M", M[:], [128, NCH, TPB], BF16, [Mk(c) for c in range(NCH)])
            MARK(('WO', p_i, l if 'WO' != 'Y' else -1))
            ss_bank = {}
            for (ti, st, nt, kind, off) in tiles:
                ss_bank[ti] = hold_bank()

            def wo_unit(cs, utiles, par):
                q = cs[0] // 2
                items = [(c, t) for c in cs for t in utiles]
                rel = (cs[-1] % 2 == 1) and (utiles[-1] is tiles[-1])
                bo = {}
                for c in cs:
                    bo[c] = proj_multi(p_i, l, "w_o", q * GC, c % 2, M, Mk, utiles)
                if rel:
                    ws_release(p_i, l, "w_o", q * GC)
                yield
                for j, (c, (ti, st, nt, kind, off)) in enumerate(items):
                    si = par * 2 + j
                    b1 = bo[c][ti]
                    ACT(O[:, c, off:off + nt], PS[:, b1, 0:nt], AF.Identity, [psk(b1)],
                        [Rk(2 * c), Rk(2 * c + 1)])
                    ACT(tb(si, nt), PS[:, b1, 0:nt], AF.Square, [psk(b1)], [tbk(si)])
                    MM(PS[:, ss_bank[ti], 0:nt], ONESB[:], tb(si, nt), c == 0, c == NCH - 1,
                       [tbk(si), ("ONESB",)], [psk(ss_bank[ti])])
                    bfree(b1)

            units = []
            n = 0
            for q in range(4):
                for (cs_, ut_) in unit_groups(q):
                    units.append(wo_unit(cs_, ut_, n % 2))
                    n += 1
            run_pipeline(units)
            for tix, (ti, st, nt, kind, off) in enumerate(tiles):
                rs_i = 12 + tix
                stats_rs(ss_bank[ti], nt, rs_i)
                bfree(ss_bank[ti])
                for c in range(NCH):
                    t_i = c % 4
                    TT(tf(t_i, nt), O[:, c, off:off + nt], tf(rs_i, nt), ALU.mult,
                       [Rk(2 * c), Rk(2 * c + 1), tfk(rs_i)], [tfk(t_i)])
                    STT(X[:, c, off:off + nt], tf(t_i, nt), vt(l, V_NPOST, c), X[:, c, off:off + nt],
                        ALU.mult, ALU.add, [tfk(t_i), Xk(c), ("VT",)], [Xk(c)])

        if p_i == 0:
            DBG("O", R32[:], [128, NCH * TPB], F32, [Rk(c) for c in range(16)])
            DBG("X", X[:], [128, NCH, TPB], F32, [Xk(c) for c in range(NCH)])
        MARK(('Y', p_i, l if 'Y' != 'Y' else -1))
        for (ti, st, nt, kind, off) in tiles:
            for r0 in range(0, nt, 128):
                nr = min(128, nt - r0)
                slot = (r0 // 128) % 2
                for half in range(2):
                    b = bank()
                    for j in range(4):
                        c = half * 4 + j
                        TR(PS[0:nr, b, j * 128:(j + 1) * 128], X[:, c, off + r0: off + r0 + nr], IDF[:],
                           [Xk(c), ("IDF",)], [psk(b)])
                    ACT(ORW[0:nr, slot, half * 512:(half + 1) * 512], PS[0:nr, b, :], AF.Identity,
                        [psk(b)], [("ORW", slot)])
                    bfree(b)
                DMA("sp", yout[st + r0: st + r0 + nr, :], ORW[0:nr, slot, :], misc_sem(), [("ORW", slot)], [])

    for e in Tracker.ENGS:
        v = 0
        for I in T.lists[e]:
            if I.inc and not I.is_dma:
                v += 1
            I.val = v

    def emit(engname, eh):
        waited = {}
        for I in T.lists[engname]:
            req = {}
            for d in I.deps:
                if d.is_dma:
                    key = ("dma", d.sem)
                    val = d.semval
                else:
                    if d.eng == "pe" and engname == "pe" and not I.is_dma:
                        continue
                    key = ("eng", d.eng)
                    val = d.val
                if val > req.get(key, 0):
                    req[key] = val
            for key, val in req.items():
                if waited.get(key, 0) >= val:
                    continue
                eh.wait_ge(sems[key], val)
                waited[key] = val
            ins = I.fn(eh)
            if I.is_dma:
                ins.then_inc(sems[("dma", I.sem)], 16)
            elif I.inc:
                ins.then_inc(sems[("eng", engname)], 1)
        if engname == "sp":
            for name, lst in T.dma_sems.items():
                eh.wait_ge(sems[("dma", name)], 16 * len(lst))

    with es:
        with nc.Block() as block:
            @block.tensor
            def _(e):
                emit("pe", e)

            @block.scalar
            def _(e):
                emit("act", e)

            @block.vector
            def _(e):
                emit("dve", e)

            @block.gpsimd
            def _(e):
                emit("pool", e)

            @block.sync
            def _(e):
                emit("sp", e)
    stats = {e: len(T.lists[e]) for e in Tracker.ENGS}
    stats['marks'] = marks
    return nc, stats


def pack_inputs(inp):
    f = lambda a: np.ascontiguousarray(np.asarray(a, dtype=np.float32))
    vec_rows = []
    for l in range(NLAYERS):
        rows = [inp["norm_pre"][l][None], inp["norm_post"][l][None], np.asarray(inp["b_in"][l]).reshape(12, D),
                inp["conv_a_w"][l], inp["conv_a_b"][l][None], inp["ln_a_g"][l][None], inp["ln_a_b"][l][None],
                inp["conv_b_w"][l], inp["conv_b_b"][l][None], inp["rg_b_a"][l][None], inp["rg_b_x"][l][None],
                inp["rg_lambda"][l][None], inp["conv_c_w"][l]]
        vec_rows.append(np.concatenate([np.asarray(r, dtype=np.float32) for r in rows], axis=0))
    vecs = np.zeros((256, D), np.float32)
    vv = np.concatenate(vec_rows, axis=0)
    vecs[:vv.shape[0]] = vv
    shared = {
        "vecs": vecs, "ident": np.eye(128, dtype=np.float32),
        "w_in": f(inp["w_in"]), "w_a_out": f(inp["w_a_out"]), "w_b_out": f(inp["w_b_out"]),
        "w_c_out": f(inp["w_c_out"]), "w_o": f(inp["w_o"]),
        "rg_w_a": f(inp["rg_w_a"]), "rg_w_x": f(inp["rg_w_x"]),
    }
    xp = np.asarray(inp["x_prompt"], np.float32)
    xs = np.asarray(inp["x_sample"], np.float32)
    sa = np.asarray(inp["state_conv_a"], np.float32)
    sbb = np.asarray(inp["state_conv_b"], np.float32)
    sl = np.asarray(inp["state_lru"], np.float32)
    sc = np.asarray(inp["state_conv_c"], np.float32)
    maps = []
    for b in range(NCORES):
        s0, s1 = b * NSEQ, (b + 1) * NSEQ
        xin = np.concatenate([xp[b], xs[s0:s1].reshape(NSEQ * 4, D)], axis=0)
        sa_in = np.ascontiguousarray(sa[:, s0:s1].transpose(0, 2, 1, 3))
        sbc = np.concatenate([sbb[:, s0:s1].transpose(0, 2, 1, 3).reshape(NLAYERS, 48, D),
                              sl[:, s0:s1],
                              sc[:, s0:s1].transpose(0, 2, 1, 3).reshape(NLAYERS, 32, D)], axis=1)
        m = dict(shared)
        m["xin"] = np.ascontiguousarray(xin)
        m["sa_in"] = sa_in
        m["sbc_in"] = np.ascontiguousarray(sbc)
        maps.append(m)
    return maps


def unpack_outputs(results, nl=NLAYERS):
    y_p = np.zeros((NCORES, 2048, D), np.float32)
    y_s = np.zeros((NCORES * NSEQ, 4, D), np.float32)
    pa = np.zeros((NLAYERS, NCORES, 30, D), np.float32)
    pb = np.zeros((NLAYERS, NCORES, 3, D), np.float32)
    ph = np.zeros((NLAYERS, NCORES, D), np.float32)
    pc = np.zeros((NLAYERS, NCORES, 2, D), np.float32)
    sa = np.zeros((NLAYERS, NCORES * NSEQ, 30, D), np.float32)
    sb_ = np.zeros((NLAYERS, NCORES * NSEQ, 3, D), np.float32)
    sh = np.zeros((NLAYERS, NCORES * NSEQ, D), np.float32)
    sc = np.zeros((NLAYERS, NCORES * NSEQ, 2, D), np.float32)
    for b in range(NCORES):
        r = results[b]
        s0, s1 = b * NSEQ, (b + 1) * NSEQ
        y = np.asarray(r["yout"])
        y_p[b] = y[:2048]
        y_s[s0:s1] = y[2048:].reshape(NSEQ, 4, D)
        st = np.asarray(r["st_out"])
        old = np.asarray(r["sa_old"])
        pa[:, b] = st[:, 0:30]
        pb[:, b] = st[:, 30:33]
        ph[:, b] = st[:, 33]
        pc[:, b] = st[:, 34:36]
        sa[:, s0:s1, 0:26] = old.transpose(0, 2, 1, 3)
        sa[:, s0:s1, 26:30] = st[:, 36:100].reshape(NLAYERS, NSEQ, 4, D)
        sb_[:, s0:s1] = st[:, 100:148].reshape(NLAYERS, NSEQ, 3, D)
        sh[:, s0:s1] = st[:, 148:164]
        sc[:, s0:s1] = st[:, 164:196].reshape(NLAYERS, NSEQ, 2, D)
    return (y_p, y_s, pa, pb, ph, pc, sa, sb_, sh, sc)


def kernel(**inputs):
    maps = pack_inputs(inputs)
    nc, _ = build_program()
    res = run_bass_kernel_spmd(nc, maps, core_ids=list(range(NCORES)))
    return unpack_outputs(res.results)
```
